# Optimizing a Trainium2 kernel written in Bass

```python
import jax, jax.numpy as jnp
from jax import lax
import numpy as np


D_MODEL = 1024
BATCH = 32
SEQ = 2048
DEPTH = 1

CHUNK = 64
N_META = 16
CONV_WIDTH = D_MODEL
CONV_K = 3
SB_HEADS = 16
SB_HEAD_DIM = 64
SB_WIDTH = SB_HEADS * SB_HEAD_DIM
N_BRANCH = 2
D_FF = -(-8 * D_MODEL // (3 * 256)) * 256
Q_BLOCK = 128
EPS = 1e-6
SPLITS = (CONV_WIDTH, 2 * CONV_WIDTH, 3 * CONV_WIDTH,
          3 * CONV_WIDTH + SB_WIDTH, 3 * CONV_WIDTH + 2 * SB_WIDTH, 3 * CONV_WIDTH + 3 * SB_WIDTH)
IN_COLS = 3 * CONV_WIDTH + 3 * SB_WIDTH + N_BRANCH * D_MODEL

kernel_name = "hybrid_shortconv_stickbreaking_gated_block"


def rmsnorm(x, g):
    xf = x.astype(jnp.float32)
    y = xf * lax.rsqrt(jnp.mean(xf * xf, axis=-1, keepdims=True) + EPS)
    return (y * g.astype(jnp.float32)).astype(x.dtype)


def short_conv(u, w):
    c = u.shape[-1]
    return lax.conv_general_dilated(
        u, w[:, None, :].astype(u.dtype), window_strides=(1,),
        padding=[(CONV_K - 1, 0)],
        dimension_numbers=('NWC', 'WIO', 'NWC'),
        feature_group_count=c)


def stick_breaking(q, k, v):
    lp = q.shape[2]
    scale = SB_HEAD_DIM ** -0.5
    outs = []
    for i in range(lp // Q_BLOCK):
        q0 = i * Q_BLOCK
        kend = q0 + Q_BLOCK
        qb = q[:, :, q0:kend].astype(jnp.float32)
        kb = k[:, :, :kend].astype(jnp.float32)
        vb = v[:, :, :kend].astype(jnp.float32)
        z = jnp.einsum('bhqd,bhkd->bhqk', qb, kb) * scale
        t_idx = q0 + jnp.arange(Q_BLOCK)[:, None]
        s_idx = jnp.arange(kend)[None, :]
        causal = s_idx < t_idx
        log_beta = jax.nn.log_sigmoid(z)
        log_1m_beta = jnp.where(causal, log_beta - z, 0.0)
        after = lax.cumsum(log_1m_beta, axis=3, reverse=True) - log_1m_beta
        a = jnp.where(causal, jnp.exp(log_beta + after), 0.0)
        outs.append(jnp.einsum('bhqk,bhkd->bhqd', a, vb))
    return jnp.concatenate(outs, axis=2).astype(v.dtype)


def setup_inputs(seed: int = 0) -> dict:
    key = jax.random.key(seed)
    ks = jax.random.split(key, 16)
    nrm = jax.random.normal
    x = nrm(ks[0], (BATCH, SEQ, D_MODEL), jnp.float32)
    meta_tokens = nrm(ks[1], (N_META, D_MODEL), jnp.float32)
    norm1_g = 1.0 + 0.02 * nrm(ks[2], (DEPTH, D_MODEL), jnp.float32)
    w_in = nrm(ks[3], (DEPTH, D_MODEL, IN_COLS), jnp.float32) * D_MODEL ** -0.5
    b_gate = 0.02 * nrm(ks[4], (DEPTH, N_BRANCH * D_MODEL), jnp.float32)
    conv_w = nrm(ks[5], (DEPTH, CONV_K, CONV_WIDTH), jnp.float32) * CONV_K ** -0.5
    w_conv_out = nrm(ks[6], (DEPTH, CONV_WIDTH, D_MODEL), jnp.float32) * CONV_WIDTH ** -0.5
    q_norm_g = 1.0 + 0.02 * nrm(ks[7], (DEPTH, SB_HEAD_DIM), jnp.float32)
    k_norm_g = 1.0 + 0.02 * nrm(ks[8], (DEPTH, SB_HEAD_DIM), jnp.float32)
    w_sb_out = nrm(ks[9], (DEPTH, SB_WIDTH, D_MODEL), jnp.float32) * SB_WIDTH ** -0.5
    w_o = nrm(ks[10], (DEPTH, D_MODEL, D_MODEL), jnp.float32) * D_MODEL ** -0.5
    norm2_g = 1.0 + 0.02 * nrm(ks[11], (DEPTH, D_MODEL), jnp.float32)
    w_ffn_in = nrm(ks[12], (DEPTH, D_MODEL, 2 * D_FF), jnp.float32) * D_MODEL ** -0.5
    w_ffn_out = nrm(ks[13], (DEPTH, D_FF, D_MODEL), jnp.float32) * D_FF ** -0.5
    return {"x": x, "meta_tokens": meta_tokens, "norm1_g": norm1_g, "w_in": w_in,
            "b_gate": b_gate, "conv_w": conv_w, "w_conv_out": w_conv_out,
            "q_norm_g": q_norm_g, "k_norm_g": k_norm_g, "w_sb_out": w_sb_out,
            "w_o": w_o, "norm2_g": norm2_g, "w_ffn_in": w_ffn_in, "w_ffn_out": w_ffn_out}


def reference(x, meta_tokens, norm1_g, w_in, b_gate, conv_w, w_conv_out,
              q_norm_g, k_norm_g, w_sb_out, w_o, norm2_g, w_ffn_in, w_ffn_out):
    bsz = x.shape[0]
    meta = jnp.broadcast_to(meta_tokens[None].astype(x.dtype), (bsz, N_META, x.shape[-1]))
    h = jnp.concatenate([meta, x], axis=1)
    seq_len = h.shape[1]
    pad = (-seq_len) % Q_BLOCK
    for l in range(DEPTH):
        n = rmsnorm(h, norm1_g[l])
        proj = n @ w_in[l]
        xb, xc, xu, q, k, v, g = jnp.split(proj, SPLITS, axis=-1)

        y_conv = (xb * short_conv(xc * xu, conv_w[l])) @ w_conv_out[l]

        def heads(t, gain):
            t = t.reshape(bsz, seq_len, SB_HEADS, SB_HEAD_DIM)
            if gain is not None:
                t = rmsnorm(t, gain)
            t = jnp.transpose(t, (0, 2, 1, 3))
            return jnp.pad(t, ((0, 0), (0, 0), (0, pad), (0, 0)))
        o = stick_breaking(heads(q, q_norm_g[l]), heads(k, k_norm_g[l]), heads(v, None))
        o = jnp.transpose(o[:, :, :seq_len], (0, 2, 1, 3)).reshape(bsz, seq_len, SB_WIDTH)
        y_sb = o @ w_sb_out[l]

        gates = jax.nn.sigmoid(g + b_gate[l]).reshape(bsz, seq_len, N_BRANCH, -1)
        merged = gates[:, :, 0] * y_conv + gates[:, :, 1] * y_sb
        h = h + merged @ w_o[l]

        n2 = rmsnorm(h, norm2_g[l])
        gate, up = jnp.split(n2 @ w_ffn_in[l], 2, axis=-1)
        h = h + (jax.nn.silu(gate) * up) @ w_ffn_out[l]
    return h[:, N_META:]
```

```python
from contextlib import ExitStack
import numpy as np
import concourse.bass as bass
import concourse.mybir as mybir
from concourse.bass_utils import run_bass_kernel_spmd

F32 = mybir.dt.float32
BF16 = mybir.dt.bfloat16
AF = mybir.ActivationFunctionType
ALU = mybir.AluOpType


class Buf:
    __slots__ = ("name", "w", "r")

    def __init__(self, name):
        self.name = name
        self.w = None
        self.r = []


class DSem:
    __slots__ = ("name", "sem", "count")

    def __init__(self, name):
        self.name = name
        self.sem = None
        self.count = 0


class Op:
    __slots__ = ("eng", "fn", "deps", "waits", "mark", "count", "dsem", "dval", "pos", "idx")


ENGS = ("pe", "act", "dve", "pool", "sp")
QUEUE_ENG = {"sync": "sp", "gpsimd": "pool", "scalar": "act"}


class Sched:
    def __init__(self, nc):
        self.nc = nc
        self.stack = ExitStack()
        self.ops = []
        self.eng_ops = {e: [] for e in ENGS}
        self.dsems = []

    def sbuf(self, name, shape, dtype):
        return self.stack.enter_context(self.nc.sbuf_tensor("sb_" + name, list(shape), dtype))

    def psum(self, name, shape, dtype):
        return self.stack.enter_context(self.nc.psum_tensor("ps_" + name, list(shape), dtype))

    def buf(self, name):
        return Buf(name)

    def dsem(self, name):
        d = DSem(name)
        self.dsems.append(d)
        return d

    def _add(self, eng, fn, reads, writes, dsem=None):
        op = Op()
        op.eng = eng
        op.fn = fn
        op.idx = len(self.ops)
        op.pos = len(self.eng_ops[eng])
        op.mark = False
        op.count = 0
        op.waits = []
        op.dsem = dsem
        op.dval = 0
        if dsem is not None:
            dsem.count += 16
            op.dval = dsem.count
        deps = []
        for b in reads:
            if b.w is not None:
                deps.append((b.w, True))
        for b in writes:
            if b.w is not None:
                deps.append((b.w, False))
            for r in b.r:
                if r is not op:
                    deps.append((r, False))
        for b in writes:
            b.w = op
            b.r = []
        for b in reads:
            if b.w is not op:
                b.r.append(op)
        op.deps = deps
        self.ops.append(op)
        self.eng_ops[eng].append(op)
        return op

    def pe(self, fn, reads=(), writes=()):
        return self._add("pe", fn, reads, writes)

    def act(self, fn, reads=(), writes=()):
        return self._add("act", fn, reads, writes)

    def dve(self, fn, reads=(), writes=()):
        return self._add("dve", fn, reads, writes)

    def pool(self, fn, reads=(), writes=()):
        return self._add("pool", fn, reads, writes)

    def dma(self, queue, fn, dsem, reads=(), writes=()):
        return self._add(QUEUE_ENG[queue], fn, reads, writes, dsem=dsem)

    def finish(self):
        nc = self.nc
        seen_pos = {e: {f: -1 for f in ENGS} for e in ENGS}
        seen_ds = {e: {} for e in ENGS}
        for op in self.ops:
            e = op.eng
            best = {}
            bestd = {}
            for (y, raw) in op.deps:
                if y.dsem is not None:
                    d = y.dsem
                    if seen_ds[e].get(d, 0) >= y.dval:
                        continue
                    if bestd.get(d, 0) < y.dval:
                        bestd[d] = y.dval
                    continue
                f = y.eng
                if f == e and op.dsem is None:
                    if e == "pe" or not raw:
                        continue
                if y.pos <= seen_pos[e][f]:
                    continue
                if f not in best or best[f].pos < y.pos:
                    best[f] = y
            for f, y in best.items():
                y.mark = True
                seen_pos[e][f] = y.pos
                op.waits.append(("c", y))
            for d, v in bestd.items():
                seen_ds[e][d] = v
                op.waits.append(("d", d, v))
        sems = {}
        for e in ENGS:
            sems[e] = self.stack.enter_context(nc.semaphore("sem_" + e))
            c = 0
            for op in self.eng_ops[e]:
                if op.mark:
                    c += 1
                    op.count = c
        for d in self.dsems:
            d.sem = self.stack.enter_context(nc.semaphore("ds_" + d.name))
        self.n_marks = {e: sum(1 for o in self.eng_ops[e] if o.mark) for e in ENGS}
        block = self.stack.enter_context(nc.Block())

        def emit(engname, eh):
            for op in self.eng_ops[engname]:
                for w in op.waits:
                    if w[0] == "c":
                        y = w[1]
                        eh.wait_ge(sems[y.eng], y.count)
                    else:
                        eh.wait_ge(w[1].sem, w[2])
                ins = op.fn(eh)
                if op.dsem is not None:
                    ins.then_inc(op.dsem.sem, 16)
                elif op.mark:
                    ins.then_inc(sems[engname], 1)
            if engname == "sp":
                for d in self.dsems:
                    if d.count > 0:
                        eh.wait_ge(d.sem, d.count)

        @block.tensor
        def _(eh):
            emit("pe", eh)

        @block.scalar
        def _(eh):
            emit("act", eh)

        @block.vector
        def _(eh):
            emit("dve", eh)

        @block.gpsimd
        def _(eh):
            emit("pool", eh)

        @block.sync
        def _(eh):
            emit("sp", eh)

        self.stack.close()


D = 1024
SEQ = 2048
NMETA = 16
NH = 16
DH = 64
DFF = 2816
NFF = DFF // 128
C = 512
NCH = SEQ // C
NSEQ = 4
EPS = 1e-6
NEG = -30000.0
PW = 4096
NPAN = 41
NW = 3
SLOT_MM = 3
P_G, P_CONV, P_WCO, P_Q, P_K, P_V, P_WSB, P_WO, P_FIN, P_FOUT = 0, 4, 12, 14, 16, 18, 20, 22, 24, 35
V_G1, V_G2, V_BG, V_CW, V_GQ, V_GK, NVEC = 0, 8, 16, 32, 56, 57, 58
C_ID, C_TRI, C_ONE, C_BLK, C_MSK, NCONST = 0, 128, 256, 384, 512, 512 + 4 * 512


def _host_consts():
    c = np.zeros((128, NCONST), np.float32)
    j = np.arange(128)[:, None]
    s = np.arange(128)[None, :]
    c[:, C_ID:C_ID + 128] = (j == s)
    c[:, C_TRI:C_TRI + 128] = -(j >= s).astype(np.float32)
    c[:, C_ONE:C_ONE + 128] = -1.0
    c[:, C_BLK:C_BLK + 128] = ((j // 64) == (s // 64))
    ql = np.arange(512)[None, :]
    for r in range(4):
        c[:, C_MSK + r * 512:C_MSK + (r + 1) * 512] = np.where(128 * r + j < ql, 0.0, NEG)
    return c


def _host_panels(w_in, w_conv_out, w_sb_out, w_o, w_ffn_in, w_ffn_out):
    pans = np.zeros((NPAN, 128, PW), np.float32)

    def colpanel(W, cols):
        sub = W[:, cols]
        n = sub.shape[1]
        return sub.reshape(8, 128, n).transpose(1, 0, 2).reshape(128, 8 * n)

    def put(i, a):
        pans[i, :, :a.shape[1]] = a

    ar = np.arange
    for pi in range(4):
        put(P_G + pi, colpanel(w_in, 6144 + pi * 512 + ar(512)))
    for j in range(8):
        cols = np.concatenate([j * 128 + ar(128), 1024 + j * 128 + ar(128), 2048 + j * 128 + ar(128)])
        put(P_CONV + j, colpanel(w_in, cols))
    for pi in range(2):
        put(P_WCO + pi, colpanel(w_conv_out, pi * 512 + ar(512)))
        put(P_Q + pi, colpanel(w_in, 3072 + pi * 512 + ar(512)))
        put(P_K + pi, colpanel(w_in, 4096 + pi * 512 + ar(512)))
        put(P_V + pi, colpanel(w_in, 5120 + pi * 512 + ar(512)))
        put(P_WSB + pi, colpanel(w_sb_out, pi * 512 + ar(512)))
        put(P_WO + pi, colpanel(w_o, pi * 512 + ar(512)))
    for i in range(11):
        cols = np.concatenate([(2 * i) * 128 + ar(128), DFF + (2 * i) * 128 + ar(128),
                               (2 * i + 1) * 128 + ar(128), DFF + (2 * i + 1) * 128 + ar(128)])
        put(P_FIN + i, colpanel(w_ffn_in, cols))
    for pi in range(6):
        nk = min(4, NFF - 4 * pi)
        blk = w_ffn_out[4 * pi * 128:(4 * pi + nk) * 128, :]
        a = blk.reshape(nk, 128, 2, 512).transpose(1, 0, 2, 3).reshape(128, nk * 2 * 512)
        put(P_FOUT + pi, a)
    return pans


def _host_vecs(norm1_g, norm2_g, b_gate, conv_w, q_norm_g, k_norm_g):
    v = np.zeros((128, NVEC), np.float32)
    v[:, V_G1:V_G1 + 8] = norm1_g.reshape(8, 128).T
    v[:, V_G2:V_G2 + 8] = norm2_g.reshape(8, 128).T
    v[:, V_BG:V_BG + 16] = b_gate.reshape(16, 128).T
    v[:, V_CW:V_CW + 24] = conv_w.reshape(3, 8, 128).transpose(2, 1, 0).reshape(128, 24)
    v[:, V_GQ] = np.tile(q_norm_g.reshape(64), 2)
    v[:, V_GK] = np.tile(k_norm_g.reshape(64), 2)
    return v


def build_program(nseq=NSEQ, nch=NCH, debug=False):
    nc = bass.Bass("TRN2", target_bir_lowering=False)
    x_d = nc.dram_tensor("x", [nseq, SEQ, D], F32, kind="ExternalInput").ap()
    meta_d = nc.dram_tensor("meta", [NMETA, D], F32, kind="ExternalInput").ap()
    vecs_d = nc.dram_tensor("vecs", [128, NVEC], F32, kind="ExternalInput").ap()
    consts_d = nc.dram_tensor("consts", [128, NCONST], F32, kind="ExternalInput").ap()
    wpan_d = nc.dram_tensor("wpan", [NPAN, 128, PW], F32, kind="ExternalInput").ap()
    out_d = nc.dram_tensor("out", [nseq, SEQ, D], F32, kind="ExternalOutput").ap()
    wscr_d = nc.dram_tensor("wscr", [NPAN, 128, PW], BF16, kind="Internal").ap()
    dbg_d = {}
    if debug:
        for nm, shp in (("d_nT", [128, 8, C]), ("d_qT", [128, 8, C]), ("d_KT", [128, 8, NMETA + SEQ]),
                        ("d_V", [128, 16, D]), ("d_uT", [128, 8, C]), ("d_sg", [128, 16, C]),
                        ("d_oT", [128, 8, C]), ("d_Vm", [NMETA, D])):
            dbg_d[nm] = nc.dram_tensor(nm, shp, BF16, kind="ExternalOutput").ap()

    S = Sched(nc)
    cst = S.sbuf("cst", [128, NCONST], BF16)
    vecs = S.sbuf("vecs", [128, NVEC], F32)
    Xn = [S.sbuf(f"Xn{i}", [128, D], F32) for i in range(2)]
    H = S.sbuf("H", [128, 4, D], F32)
    XS = [S.sbuf(f"XS{i}", [128, D], BF16) for i in range(2)]
    bufA = S.sbuf("bufA", [128, 8, C], BF16)
    mT = S.sbuf("mT", [128, 8, C], BF16)
    actS = [S.sbuf(f"actS{i}", [128, 4, C], BF16) for i in range(2)]
    bufB = S.sbuf("bufB", [128, 8, C], BF16)
    sg = S.sbuf("sg", [128, 16, C], BF16)
    qT = S.sbuf("qT", [128, 8, C], BF16)
    KT = S.sbuf("KT", [128, 8, NMETA + SEQ], BF16)
    Vc = S.sbuf("Vc", [128, 16, D], BF16)
    Vm = S.sbuf("Vm", [NMETA, D], BF16)
    U = S.sbuf("U", [128, 10240], BF16)
    W = [S.sbuf(f"W{i}", [128, PW], BF16) for i in range(NW)]
    stat = S.sbuf("stat", [128, 4, 4], F32)
    hs = S.sbuf("hs", [128, 8, 2], F32)
    hs0 = S.sbuf("hs0", [128, 8, 2], F32)
    PS = S.psum("PS", [128, 8 * 512], F32)

    ident = cst[:, C_ID:C_ID + 128]
    triN = cst[:, C_TRI:C_TRI + 128]
    onesN = cst[:, C_ONE:C_ONE + 128]
    blk = cst[:, C_BLK:C_BLK + 128]
    msk = cst[:, C_MSK:C_MSK + 2048].rearrange("p (r q) -> p r q", r=4)

    def bank(b, n=1):
        return PS[:, b * 512:(b + n) * 512]

    def bank_bf(b):
        return PS[:, b * 512:(b + 1) * 512].bitcast(BF16)

    def u_e(u):
        return U[:, u * 2048:(u + 1) * 2048].bitcast(F32)
    def u_sp(u):
        return U[:, 4096 + u * 1024:4096 + (u + 1) * 1024]
    def u_a(u):
        return U[:, 6144 + u * 1024:6144 + (u + 1) * 1024]
    def u_S(i):
        return U[:, 8192 + i * 1024:8192 + (i + 1) * 1024]
    def u_f32(slot):
        return U[:, slot * 1040:(slot + 1) * 1040].bitcast(F32)
    xc_s = [u_f32(0), u_f32(1)]
    uu_s = [u_f32(2), u_f32(3)]
    y_s = [u_f32(4), u_f32(5)]
    rs_s = [u_f32(6), u_f32(7)]
    sq_s = [U[:, 8 * 1040 + i * 512:8 * 1040 + (i + 1) * 512] for i in range(2)]
    tmp_s = [u_f32(2), u_f32(3)]
    sl_s = [u_f32(0), u_f32(1)]

    Bcst, Bvec = S.buf("cst"), S.buf("vecs")
    BXn = [S.buf("Xn0"), S.buf("Xn1")]
    BH = [S.buf(f"H{t}") for t in range(4)]
    BXS = [S.buf("XS0"), S.buf("XS1")]
    BA = [S.buf(f"A{k}") for k in range(8)]
    BB = [S.buf(f"B{k}") for k in range(8)]
    Bsg = [S.buf(f"sg{k}") for k in range(16)]
    BqT = [S.buf(f"qT{k}") for k in range(8)]
    BKm = [S.buf(f"KTm{k}") for k in range(8)]
    BKT = [[S.buf(f"KT{k}_{c}") for c in range(NCH)] for k in range(8)]
    BV = [S.buf(f"V{t}") for t in range(16)]
    BVm = S.buf("Vm")
    BW = [S.buf(f"W{i}") for i in range(NW)]
    Bm = [S.buf(f"m{k}") for k in range(8)]
    BactS = [S.buf("actS0"), S.buf("actS1")]
    Bstat = [S.buf(f"stat{i}") for i in range(4)]
    Bhs = [S.buf(f"hs{k}") for k in range(8)]
    Bhs0 = S.buf("hs0")
    PB = [S.buf(f"pb{i}") for i in range(8)]
    BU = S.buf("U")
    Be = [S.buf("e0"), S.buf("e1")]
    Bsp = [S.buf("sp0"), S.buf("sp1")]
    Ba = [S.buf("a0"), S.buf("a1")]
    BS = [S.buf("S0"), S.buf("S1")]
    Bxc = [S.buf("xc0"), S.buf("xc1")]
    Buu = [S.buf("uu0"), S.buf("uu1")]
    By = [S.buf("y0"), S.buf("y1")]
    Brs = [S.buf("rs0"), S.buf("rs1")]
    Bsq = [S.buf("sq0"), S.buf("sq1")]
    Bwscr = [S.buf(f"wscr{i}") for i in range(NPAN)]
    P1_BUFS = Bxc + Buu + By + Brs + Bsq
    P2_BUFS = Be + Bsp + Ba + BS

    def fence(olds, news):
        acc = []
        for o in olds:
            if o.w is not None:
                acc.append(o.w)
            acc.extend(o.r)
        for n in news:
            n.r = n.r + acc

    ds_c = S.dsem("c")
    ds_c2 = S.dsem("c2")
    ds_xn = [S.dsem("xn0"), S.dsem("xn1")]
    ds_h = S.dsem("h")
    ds_w = [S.dsem(f"w{i}") for i in range(NW)]
    ds_cv = [S.dsem("cv0"), S.dsem("cv1")]
    ds_scr = [S.dsem(f"scr{i}") for i in range(NW)]
    ds_out = [S.dsem(f"out{i}") for i in range(4)]
    ds_dbg = S.dsem("dbg")

    S.dma("sync", lambda e: e.dma_start(out=vecs[:], in_=vecs_d), ds_c, writes=[Bvec])
    cstage = H.rearrange("p a b -> p (a b)")[:, 0:NCONST]
    S.dma("sync", lambda e: e.dma_start(out=cstage, in_=consts_d), ds_c2, writes=BH)
    S.dve(lambda e: e.tensor_copy(out=cst[:], in_=cstage), reads=BH, writes=[Bcst])
    S.pool(lambda e: e.memset(hs0[:], 0.0), writes=[Bhs0])

    stg = [H.rearrange("p a b -> p (a b)"), sg.rearrange("p a b -> p (a b)").bitcast(F32)]
    Bstg = [BH, Bsg]
    def cv_load(i):
        s2 = i % 2
        S.dma("sync", lambda e: e.dma_start(out=stg[s2], in_=wpan_d[i]), ds_cv[s2], writes=Bstg[s2])

    def cv_cast_store(i):
        s2 = i % 2
        s4 = i % NW
        if i % 3 == 0:
            S.dve(lambda e: e.tensor_copy(out=W[s4][:], in_=stg[s2]), reads=Bstg[s2], writes=[BW[s4]])
        else:
            S.act(lambda e: e.activation(out=W[s4][:], in_=stg[s2], func=AF.Copy), reads=Bstg[s2], writes=[BW[s4]])
        S.dma("sync", lambda e: e.dma_start(out=wscr_d[i], in_=W[s4][:]), ds_scr[s4], reads=[BW[s4]], writes=[Bwscr[i]])

    cv_load(0)
    for i in range(NPAN):
        if i + 1 < NPAN:
            cv_load(i + 1)
        cv_cast_store(i)

    serial_stream = (list(range(P_G, P_G + 4)) + list(range(P_CONV, P_CONV + 8)) + [P_WCO, P_WCO + 1]
                     + [P_Q, P_Q + 1, P_K, P_K + 1, P_V, P_V + 1])
    ffn_stream = []
    for gi in range(6):
        ffn_stream += [P_FIN + 2 * gi] + ([P_FIN + 2 * gi + 1] if 2 * gi + 1 < 11 else [])
        if gi >= 1:
            ffn_stream += [P_FOUT + gi - 1]
    ffn_stream += [P_FOUT + 5]
    dense_stream = [P_WO, P_WO + 1] + ffn_stream
    meta_stream = list(range(P_CONV, P_CONV + 8)) + [P_K, P_K + 1, P_V, P_V + 1]
    stream = list(meta_stream)
    for ci in range(nseq * nch):
        stream += serial_stream + (dense_stream if ci > 0 else []) + [P_WSB, P_WSB + 1]
    stream += dense_stream
    st = {"issued": 0, "next": 0}

    def issue_to(n):
        while st["issued"] < min(n, len(stream)):
            i = st["issued"]
            pid = stream[i]
            s4 = i % NW
            S.dma("sync", lambda e, pid=pid, s4=s4: e.dma_start(out=W[s4][:], in_=wscr_d[pid]), ds_w[s4],
                  reads=[Bwscr[pid]], writes=[BW[s4]])
            st["issued"] += 1

    def next_panel(expect):
        i = st["next"]
        assert stream[i] == expect, (i, stream[i], expect)
        issue_to(i + NW)
        st["next"] += 1
        s4 = i % NW
        return W[s4], BW[s4]

    rot = {"bank": 0, "xn": 0, "xs": 0, "stat": 0, "p1": 0}

    def alloc_banks(n):
        if rot["bank"] + n > 8:
            rot["bank"] = 0
        b = rot["bank"]
        rot["bank"] = (b + n) % 8
        return b

    def nxt(key, m=2):
        v = rot[key]
        rot[key] = (v + 1) % m
        return v

    def norm_transpose(src, src_bufs, gcol, dst, dst_bufs, off, rows):
        si = nxt("stat", 4)
        xi = nxt("xs")
        stt = stat[:, si, :]
        S.pool(lambda e: e.memset(stt[:rows, 0:1], 0.0), writes=[Bstat[si]])
        S.act(lambda e: e.activation(out=XS[xi][:rows, :], in_=src, func=AF.Square, accum_out=stt[:rows, 0:1]),
              reads=src_bufs + [Bstat[si]], writes=[BXS[xi], Bstat[si]])
        S.act(lambda e: e.activation(out=stt[:rows, 1:2], in_=stt[:rows, 0:1], func=AF.Ln, scale=1.0 / D, bias=EPS),
              reads=[Bstat[si]], writes=[Bstat[si]])
        S.act(lambda e: e.activation(out=stt[:rows, 2:3], in_=stt[:rows, 1:2], func=AF.Exp, scale=-0.5),
              reads=[Bstat[si]], writes=[Bstat[si]])
        S.dve(lambda e: e.tensor_scalar(out=XS[xi][:rows, :], in0=src, scalar1=stt[:rows, 2:3], scalar2=None, op0=ALU.mult),
              reads=src_bufs + [Bstat[si]], writes=[BXS[xi]])
        b = alloc_banks(1)
        pt = bank_bf(b)
        for k in range(8):
            S.pe(lambda e, k=k: e.transpose(out=pt[:, k * 128:k * 128 + rows], in_=XS[xi][:rows, k * 128:(k + 1) * 128],
                                            identity=ident[:rows, :rows]),
                 reads=[BXS[xi], Bcst], writes=[PB[b]])
        ptv = pt.rearrange("p (k t) -> p k t", k=8)[:, :, 0:rows]
        S.dve(lambda e: e.tensor_tensor(out=dst[:, :, off:off + rows], in0=ptv,
                                        in1=gcol.unsqueeze(2).to_broadcast([128, 8, rows]), op=ALU.mult),
              reads=[PB[b], Bvec], writes=dst_bufs)

    def fm_matmul(b, lhs_of_k, rhs_of_k, nk, ntok, reads):
        for k in range(nk):
            S.pe(lambda e, k=k: e.matmul(bank(b)[:, 0:ntok], lhsT=lhs_of_k(k), rhs=rhs_of_k(k), start=(k == 0), stop=(k == nk - 1)),
                 reads=reads, writes=[PB[b]])

    hn_pending = []

    def head_norm(b, gcolumn, lnbias, dst, dst_bufs, ntok):
        i = nxt("p1")
        sq, rs = sq_s[i][:, 0:ntok], rs_s[i][:, 0:ntok]
        S.act(lambda e: e.activation(out=sq, in_=bank(b)[:, 0:ntok], func=AF.Square), reads=[PB[b]], writes=[Bsq[i]])
        b2 = alloc_banks(1)
        S.pe(lambda e: e.matmul(bank(b2)[:, 0:ntok], lhsT=blk, rhs=sq, start=True, stop=True), reads=[Bsq[i], Bcst], writes=[PB[b2]])

        def stage_b():
            S.act(lambda e: e.activation(out=rs, in_=bank(b2)[:, 0:ntok], func=AF.Ln, scale=1.0 / DH, bias=EPS), reads=[PB[b2]], writes=[Brs[i]])
            S.act(lambda e: e.activation(out=rs, in_=rs, func=AF.Exp, scale=-0.5, bias=lnbias), reads=[Brs[i]], writes=[Brs[i]])
            S.dve(lambda e: e.scalar_tensor_tensor(out=dst, in0=bank(b)[:, 0:ntok], scalar=gcolumn, in1=rs, op0=ALU.mult, op1=ALU.mult),
                  reads=[PB[b], Brs[i], Bvec], writes=dst_bufs)
        hn_flush()
        hn_pending.append(stage_b)

    def hn_flush():
        while hn_pending:
            hn_pending.pop(0)()

    def phase1(ntok, meta, kt_off, kt_bufs, vtile0, seq_first):
        nT = bufA
        rhsA = lambda k: nT[:, k, 0:ntok]
        if not meta:
            for pi in range(4):
                Wp, BWp = next_panel(P_G + pi)
                wv = Wp.rearrange("p (k c) -> p k c", k=8)
                for jj in range(4):
                    j = pi * 4 + jj
                    b = alloc_banks(1)
                    fm_matmul(b, lambda k, jj=jj, wv=wv: wv[:, k, jj * 128:(jj + 1) * 128], rhsA, 8, ntok, BA + [BWp])
                    S.act(lambda e, b=b, j=j: e.activation(out=sg[:, j, :], in_=bank(b), func=AF.Sigmoid, bias=vecs[:, V_BG + j:V_BG + j + 1]),
                          reads=[PB[b], Bvec], writes=[Bsg[j]])
        for j in range(8):
            Wp, BWp = next_panel(P_CONV + j)
            wv = Wp[:, 0:8 * 384].rearrange("p (k c) -> p k c", k=8)
            b0 = alloc_banks(3)
            tiles = (1, 2) if meta else (0, 1, 2)
            for t in tiles:
                fm_matmul(b0 + t, lambda k, t=t, wv=wv: wv[:, k, t * 128:(t + 1) * 128], rhsA, 8, ntok, BA + [BWp])
            i = nxt("p1")
            xc, uu, y = xc_s[i], uu_s[i], y_s[i]
            S.act(lambda e, b0=b0, xc=xc: e.activation(out=xc[:, 0:ntok], in_=bank(b0 + 1)[:, 0:ntok], func=AF.Copy), reads=[PB[b0 + 1]], writes=[Bxc[i]])
            if not meta:
                src_h, src_b = (hs0, Bhs0) if seq_first else (hs, Bhs[j])
                S.pool(lambda e, uu=uu, src_h=src_h, j=j: e.tensor_copy(out=uu[:, 0:2], in_=src_h[:, j, :]), reads=[src_b], writes=[Buu[i]])
            S.dve(lambda e, b0=b0, xc=xc, uu=uu: e.tensor_tensor(out=uu[:, 2:2 + ntok], in0=bank(b0 + 2)[:, 0:ntok], in1=xc[:, 0:ntok], op=ALU.mult),
                  reads=[PB[b0 + 2], Bxc[i]], writes=[Buu[i]])
            dst_h, dst_b = (hs0, Bhs0) if meta else (hs, Bhs[j])
            S.pool(lambda e, uu=uu, dst_h=dst_h, j=j: e.tensor_copy(out=dst_h[:, j, :], in_=uu[:, ntok:ntok + 2]), reads=[Buu[i]], writes=[dst_b])
            if meta:
                continue
            cw = lambda tap, j=j: vecs[:, V_CW + j * 3 + tap:V_CW + j * 3 + tap + 1]
            S.dve(lambda e, uu=uu, y=y, cw=cw: e.tensor_scalar(out=y[:, 0:ntok], in0=uu[:, 2:2 + ntok], scalar1=cw(2), scalar2=None, op0=ALU.mult),
                   reads=[Buu[i], Bvec], writes=[By[i]])
            S.dve(lambda e, uu=uu, y=y, cw=cw: e.scalar_tensor_tensor(out=y[:, 0:ntok], in0=uu[:, 1:1 + ntok], scalar=cw(1), in1=y[:, 0:ntok], op0=ALU.mult, op1=ALU.add),
                   reads=[Buu[i], By[i], Bvec], writes=[By[i]])
            S.dve(lambda e, uu=uu, y=y, cw=cw: e.scalar_tensor_tensor(out=y[:, 0:ntok], in0=uu[:, 0:ntok], scalar=cw(0), in1=y[:, 0:ntok], op0=ALU.mult, op1=ALU.add),
                   reads=[Buu[i], By[i], Bvec], writes=[By[i]])
            S.dve(lambda e, b0=b0, y=y, j=j: e.tensor_tensor(out=bufB[:, j, 0:ntok], in0=bank(b0)[:, 0:ntok], in1=y[:, 0:ntok], op=ALU.mult),
                  reads=[PB[b0], By[i]], writes=[BB[j]])
        if not meta:
            for pi in range(2):
                Wp, BWp = next_panel(P_WCO + pi)
                wv = Wp.rearrange("p (k c) -> p k c", k=8)
                for jj in range(4):
                    j = pi * 4 + jj
                    b = alloc_banks(1)
                    fm_matmul(b, lambda k, jj=jj, wv=wv: wv[:, k, jj * 128:(jj + 1) * 128], lambda k: bufB[:, k, 0:ntok], 8, ntok, BB + [BWp])
                    S.dve(lambda e, b=b, j=j: e.tensor_tensor(out=sg[:, j, :], in0=bank(b), in1=sg[:, j, :], op=ALU.mult),
                          reads=[PB[b], Bsg[j]], writes=[Bsg[j]])
            for pi in range(2):
                Wp, BWp = next_panel(P_Q + pi)
                wv = Wp.rearrange("p (k c) -> p k c", k=8)
                for jj in range(4):
                    j = pi * 4 + jj
                    b = alloc_banks(1)
                    fm_matmul(b, lambda k, jj=jj, wv=wv: wv[:, k, jj * 128:(jj + 1) * 128], rhsA, 8, ntok, BA + [BWp])
                    head_norm(b, vecs[:, V_GQ:V_GQ + 1], float(np.log(DH ** -0.5)), qT[:, j, 0:ntok], [BqT[j]], ntok)
        for pi in range(2):
            Wp, BWp = next_panel(P_K + pi)
            wv = Wp.rearrange("p (k c) -> p k c", k=8)
            for jj in range(4):
                j = pi * 4 + jj
                b = alloc_banks(1)
                fm_matmul(b, lambda k, jj=jj, wv=wv: wv[:, k, jj * 128:(jj + 1) * 128], rhsA, 8, ntok, BA + [BWp])
                head_norm(b, vecs[:, V_GK:V_GK + 1], 0.0, KT[:, j, kt_off:kt_off + ntok], [kt_bufs[j]], ntok)
        hn_flush()
        ntt = 1 if meta else ntok // 128
        rows = ntok if meta else 128
        vb = alloc_banks(8) if not meta else alloc_banks(2)
        for pi in range(2):
            Wp, BWp = next_panel(P_V + pi)
            wv = Wp.rearrange("p (k c) -> p k c", k=8)
            for tt in range(ntt):
                b = vb + tt * 2 + pi
                for k in range(8):
                    S.pe(lambda e, b=b, k=k, tt=tt, wv=wv: e.matmul(bank(b)[0:rows, :], lhsT=nT[:, k, tt * 128:tt * 128 + rows], rhs=wv[:, k, :],
                                                                  start=(k == 0), stop=(k == 7)),
                         reads=BA + [BWp], writes=[PB[b]])
                if meta:
                    S.dve(lambda e, b=b, pi=pi: e.tensor_copy(out=Vm[:, pi * 512:(pi + 1) * 512], in_=bank(b)[0:rows, :]), reads=[PB[b]], writes=[BVm])
                elif (tt + pi) % 2 == 0:
                    S.dve(lambda e, b=b, pi=pi, tt=tt: e.tensor_copy(out=Vc[:, vtile0 + tt, pi * 512:(pi + 1) * 512], in_=bank(b)), reads=[PB[b]], writes=[BV[vtile0 + tt]])
                else:
                    S.act(lambda e, b=b, pi=pi, tt=tt: e.activation(out=Vc[:, vtile0 + tt, pi * 512:(pi + 1) * 512], in_=bank(b), func=AF.Copy), reads=[PB[b]], writes=[BV[vtile0 + tt]])
    def dense_items(bseq, c, dbanks, nxt_chunk=None):
        rr = {"i": 0}

        def nb():
            b = dbanks[rr["i"] % len(dbanks)]
            rr["i"] += 1
            return b

        for pi in range(2):
            hold = {}
            for tt in range(4):
                b = nb()

                def mm(k, b=b, pi=pi, tt=tt, hold=hold):
                    if "w" not in hold:
                        hold["w"] = next_panel(P_WO + pi)
                    Wp, BWp = hold["w"]
                    wv = Wp.rearrange("p (k c) -> p k c", k=8)
                    S.pe(lambda e: e.matmul(bank(b), lhsT=mT[:, k, tt * 128:(tt + 1) * 128], rhs=wv[:, k, :], start=(k == 0), stop=(k == 7)),
                         reads=Bm + [BWp], writes=[PB[b]])
                pe = [(lambda k=k, mm=mm: mm(k)) for k in range(8)]

                def post(b=b, pi=pi, tt=tt):
                    S.dve(lambda e: e.tensor_tensor(out=H[:, tt, pi * 512:(pi + 1) * 512], in0=bank(b), in1=H[:, tt, pi * 512:(pi + 1) * 512], op=ALU.add),
                          reads=[PB[b], BH[tt]], writes=[BH[tt]])
                yield dict(mms=pe, post=post, bank=b, flush=False, pre=None)
        def norm_items(src_of, src_bufs_of, gcol, dst, dst_bufs, loader=None):
            holds = [dict() for _ in range(4)]

            def make_a(tt):
                def pre():
                    if loader is not None:
                        loader(tt)
                    si = nxt("stat", 4)
                    xs = nxt("xs")
                    holds[tt]["xs"] = xs
                    stt = stat[:, si, :]
                    src = src_of(tt)
                    sb = src_bufs_of(tt)
                    S.dve(lambda e: e.memset(stt[:, 0:1], 0.0), writes=[Bstat[si]])
                    S.act(lambda e: e.activation(out=XS[xs][:], in_=src, func=AF.Square, accum_out=stt[:, 0:1]),
                          reads=sb + [Bstat[si]], writes=[BXS[xs], Bstat[si]])
                    S.act(lambda e: e.activation(out=stt[:, 1:2], in_=stt[:, 0:1], func=AF.Ln, scale=1.0 / D, bias=EPS), reads=[Bstat[si]], writes=[Bstat[si]])
                    S.act(lambda e: e.activation(out=stt[:, 2:3], in_=stt[:, 1:2], func=AF.Exp, scale=-0.5), reads=[Bstat[si]], writes=[Bstat[si]])
                    S.dve(lambda e: e.tensor_scalar(out=XS[xs][:], in0=src, scalar1=stt[:, 2:3], scalar2=None, op0=ALU.mult),
                          reads=sb + [Bstat[si]], writes=[BXS[xs]])
                return dict(mms=[], post=None, bank=None, flush=(tt == 0), pre=pre)

            def make_b(tt):
                b = nb()

                def mm(k):
                    xs = holds[tt]["xs"]
                    pt = bank_bf(b)
                    S.pe(lambda e: e.transpose(out=pt[:, k * 128:(k + 1) * 128], in_=XS[xs][:, k * 128:(k + 1) * 128], identity=ident),
                         reads=[BXS[xs], Bcst], writes=[PB[b]])

                def post():
                    ptv = bank_bf(b).rearrange("p (k t) -> p k t", k=8)
                    S.dve(lambda e: e.tensor_tensor(out=dst[:, :, tt * 128:(tt + 1) * 128], in0=ptv,
                                                    in1=gcol.unsqueeze(2).to_broadcast([128, 8, 128]), op=ALU.mult),
                          reads=[PB[b], Bvec], writes=dst_bufs)
                return dict(mms=[(lambda k=k: mm(k)) for k in range(8)], post=post, bank=b, flush=False, pre=None)

            for kind, tt in (("A", 0), ("A", 1), ("B", 0), ("A", 2), ("B", 1), ("A", 3), ("B", 2), ("B", 3)):
                yield make_a(tt) if kind == "A" else make_b(tt)

        yield from norm_items(lambda tt: H[:, tt, :], lambda tt: [BH[tt]], vecs[:, V_G2:V_G2 + 8], mT, Bm)
        def ffn_in_items(gi):
            nk = min(4, NFF - 4 * gi)
            sl = gi % 2
            hold = {}
            for jj in range(nk):
                for t in range(2):
                    b = nb()

                    def mm(k, b=b, jj=jj, t=t, gi=gi, hold=hold):
                        if jj % 2 == 0 and t == 0 and k == 0:
                            hold["w"] = next_panel(P_FIN + 2 * gi + jj // 2)
                        Wp, BWp = hold["w"]
                        wv = Wp.rearrange("p (k c) -> p k c", k=8)
                        col = ((jj % 2) * 2 + t) * 128
                        S.pe(lambda e: e.matmul(bank(b), lhsT=wv[:, k, col:col + 128], rhs=mT[:, k, :], start=(k == 0), stop=(k == 7)),
                             reads=Bm + [BWp], writes=[PB[b]])
                    pe = [(lambda k=k, mm=mm: mm(k)) for k in range(8)]

                    def post(b=b, jj=jj, t=t, sl=sl, hold=hold):
                        if t == 0:
                            xi = nxt("xn")
                            hold["xi"] = xi
                            tmp = Xn[xi][:, 0:C]
                            S.act(lambda e: e.activation(out=tmp, in_=bank(b), func=AF.Exp, scale=-1.0), reads=[PB[b]], writes=[BXn[xi]])
                            S.act(lambda e: e.activation(out=tmp, in_=tmp, func=AF.Ln, bias=1.0), reads=[BXn[xi]], writes=[BXn[xi]])
                            S.act(lambda e: e.activation(out=tmp, in_=tmp, func=AF.Exp, scale=-1.0), reads=[BXn[xi]], writes=[BXn[xi]])
                            S.dve(lambda e: e.tensor_tensor(out=tmp, in0=bank(b), in1=tmp, op=ALU.mult), reads=[PB[b], BXn[xi]], writes=[BXn[xi]])
                        else:
                            xi = hold["xi"]
                            tmp = Xn[xi][:, 0:C]
                            S.dve(lambda e: e.tensor_tensor(out=actS[sl][:, jj, :], in0=bank(b), in1=tmp, op=ALU.mult),
                                  reads=[PB[b], BXn[xi]], writes=[BactS[sl]])
                    yield dict(mms=pe, post=post, bank=b, flush=(gi == 0 and jj == 0 and t == 0), pre=None)
        def fout_items(gi):
            nk = min(4, NFF - 4 * gi)
            sl = gi % 2
            hold2 = {}
            for tt in range(4):
                for half in range(2):
                    b = nb()

                    def mm(kk, b=b, tt=tt, half=half, gi=gi, nk=nk, sl=sl, hold2=hold2):
                        if "w" not in hold2:
                            hold2["w"] = next_panel(P_FOUT + gi)
                        Wp, BWp = hold2["w"]
                        wv = Wp[:, 0:nk * 1024].rearrange("p (s c) -> p s c", c=512)
                        S.pe(lambda e: e.matmul(bank(b), lhsT=actS[sl][:, kk, tt * 128:(tt + 1) * 128], rhs=wv[:, kk * 2 + half, :],
                                                start=(kk == 0), stop=(kk == nk - 1)),
                             reads=[BactS[sl], BWp], writes=[PB[b]])
                    pe = [(lambda kk=kk, mm=mm: mm(kk)) for kk in range(nk)]

                    def post(b=b, tt=tt, half=half, gi=gi):
                        S.dve(lambda e: e.tensor_tensor(out=H[:, tt, half * 512:(half + 1) * 512], in0=bank(b), in1=H[:, tt, half * 512:(half + 1) * 512], op=ALU.add),
                              reads=[PB[b], BH[tt]], writes=[BH[tt]])
                        if gi == 5 and half == 1:
                            r0 = c * C + tt * 128
                            S.dma("sync", lambda e: e.dma_start(out=out_d[bseq, r0:r0 + 128, :], in_=H[:, tt, :]), ds_out[tt], reads=[BH[tt]])
                    yield dict(mms=pe, post=post, bank=b, flush=(tt == 0 and half == 0), pre=None)

        for gi in range(6):
            yield from ffn_in_items(gi)
            if gi >= 1:
                yield from fout_items(gi - 1)
        yield from fout_items(5)

        if nxt_chunk is not None:
            nb_, nc_ = nxt_chunk
            xslot = {}

            def loader(tt):
                xi = nxt("xn")
                xslot[tt] = xi
                r0 = nc_ * C + tt * 128
                S.dma("sync", lambda e: e.dma_start(out=Xn[xi][:], in_=x_d[nb_, r0:r0 + 128, :]), ds_xn[xi], writes=[BXn[xi]])
            yield from norm_items(lambda tt: Xn[xslot[tt]][:], lambda tt: [BXn[xslot[tt]]], vecs[:, V_G1:V_G1 + 8], bufA, BA, loader)

    class DenseFeeder:
        def __init__(self, gen, n_mm, lag=1):
            self.gen = gen
            self.lag = lag
            self.left = n_mm
            self.pending = {}
            self.cur = None
            self.idx = 0

        def feed(self, n):
            while n > 0 and self.gen is not None:
                if self.cur is None:
                    try:
                        it = next(self.gen)
                    except StopIteration:
                        self.gen = None
                        self.flush()
                        return
                    if it["flush"]:
                        self.flush()
                    if it["bank"] is not None and it["bank"] in self.pending:
                        self.pending.pop(it["bank"])()
                    if it["pre"] is not None:
                        it["pre"]()
                    self.cur = it
                    self.idx = 0
                mms = self.cur["mms"]
                take = min(n, len(mms) - self.idx)
                for f in mms[self.idx:self.idx + take]:
                    f()
                self.idx += take
                n -= take
                self.left -= take
                if self.idx == len(mms):
                    if self.cur["post"] is not None:
                        while len(self.pending) >= self.lag:
                            self.pending.pop(next(iter(self.pending)))()
                        self.pending[self.cur["bank"]] = self.cur["post"]
                    self.cur = None

        def flush(self):
            for b in list(self.pending):
                self.pending.pop(b)()

        def drain(self):
            while self.gen is not None:
                self.feed(64)
            self.flush()

    N_DENSE_MM = 64 + 32 + 352 + 176

    def phase2(c, feeder):
        fence(P1_BUFS, P2_BUFS)
        blocks = [("d", 4 * c + r, r) for r in (3, 2, 1, 0)] + [("f", kt, 0) for kt in range(4 * c - 1, -1, -1)] + [("m", 0, 0)]
        nb = len(blocks)
        steps = [(j, n) for j in range(8) for n in range(nb)]
        ns = len(steps)
        single = (feeder is not None and c <= 1)
        ZD = [0, 0] if single else [0, 2]
        OB = [4, 5]

        def geom(n):
            kind, kt, r = blocks[n]
            if kind == "m":
                return kind, kt, r, NMETA, 0, 0
            return kind, kt, r, 128, NMETA + kt * 128, (128 * r if kind == "d" else 0)

        def v3(ap2):
            return ap2.rearrange("p (h q) -> p h q", h=2)

        def QK(g):
            j, n = steps[g]
            kind, kt, r, nk, koff, q0 = geom(n)
            z = ZD[g % 2]
            kb = [BKm[j]] if kind == "m" else [BKT[j][kt // 4]]
            for half in range(2):
                lo = 64 * half
                b = z + half
                S.pe(lambda e, b=b, lo=lo: e.matmul(bank(b)[0:nk, q0:C], lhsT=KT[lo:lo + 64, j, koff:koff + nk], rhs=qT[lo:lo + 64, j, q0:C],
                                                    start=True, stop=(kind != "d")),
                     reads=kb + [BqT[j]], writes=[PB[b]])
            if kind == "d":
                for half in range(2):
                    b = z + half
                    S.pe(lambda e, b=b: e.matmul(bank(b)[:, q0:q0 + 128], lhsT=ident, rhs=msk[:, 0, 0:128], start=False, stop=True, skip_group_check=True),
                         reads=[Bcst], writes=[PB[b]])

        def E1(g):
            j, n = steps[g]
            nk, q0 = geom(n)[3], geom(n)[5]
            u = g % 2
            zd = v3(bank(ZD[u], 2))
            S.act(lambda e: e.activation(out=v3(u_e(u))[0:nk, :, q0:C], in_=zd[0:nk, :, q0:C], func=AF.Exp), reads=[PB[ZD[u]], PB[ZD[u] + 1]], writes=[Be[u]])

        def SP(g):
            j, n = steps[g]
            nk, q0 = geom(n)[3], geom(n)[5]
            u = g % 2
            S.act(lambda e: e.activation(out=v3(u_sp(u))[0:nk, :, q0:C], in_=v3(u_e(u))[0:nk, :, q0:C], func=AF.Ln, bias=1.0), reads=[Be[u]], writes=[Bsp[u]])

        def TO(g):
            j, n = steps[g]
            nk, q0 = geom(n)[3], geom(n)[5]
            u = g % 2
            si = n % 2
            if n == 0:
                for i in range(2):
                    S.dve(lambda e, i=i: e.memset(u_S(i), 0.0), writes=[BS[i]])
            for half in range(2):
                b = ZD[u] + half
                S.pe(lambda e, b=b, half=half: e.matmul(bank(b)[0:nk, q0:C], lhsT=triN[0:nk, 0:nk], rhs=v3(u_sp(u))[0:nk, half, q0:C],
                                                       start=False, stop=(n == 0), skip_group_check=True),
                     reads=[Bsp[u], Bcst], writes=[PB[b]])
                if n > 0:
                    S.pe(lambda e, b=b, half=half: e.matmul(bank(b)[0:nk, q0:C], lhsT=onesN[:, 0:nk], rhs=v3(u_S(si))[:, half, q0:C],
                                                           start=False, stop=True, skip_group_check=True),
                         reads=[BS[si], Bcst], writes=[PB[b]])
            if n + 1 < nb:
                if n == 0:
                    S.dve(lambda e: e.tensor_copy(out=v3(u_S(1))[:, :, q0:C], in_=v3(u_sp(u))[:, :, q0:C]), reads=[Bsp[u]], writes=[BS[1]])
                else:
                    S.dve(lambda e: e.tensor_tensor(out=v3(u_S(1 - si))[:, :, q0:C], in0=v3(u_S(si))[:, :, q0:C], in1=v3(u_sp(u))[:, :, q0:C], op=ALU.add),
                          reads=[BS[si], Bsp[u]], writes=[BS[1 - si]])

        def E2(g):
            j, n = steps[g]
            nk, q0 = geom(n)[3], geom(n)[5]
            u = g % 2
            zd = v3(bank(ZD[u], 2))
            S.act(lambda e: e.activation(out=v3(u_a(u))[0:nk, :, q0:C], in_=zd[0:nk, :, q0:C], func=AF.Exp), reads=[PB[ZD[u]], PB[ZD[u] + 1]], writes=[Ba[u]])

        def AV(g):
            j, n = steps[g]
            kind, kt, r, nk, koff, q0 = geom(n)
            u = g % 2
            if kind == "m":
                lhs, rb = Vm[:, j * 128:(j + 1) * 128], [BVm]
            else:
                lhs, rb = Vc[:, kt, j * 128:(j + 1) * 128], [BV[kt]]
            for half in range(2):
                ob = OB[half]
                S.pe(lambda e, half=half, ob=ob: e.matmul(bank(ob)[:, q0:C], lhsT=lhs, rhs=v3(u_a(u))[0:nk, half, q0:C],
                                                         start=(n == 0), stop=(n == nb - 1), skip_group_check=True),
                     reads=rb + [Ba[u]], writes=[PB[ob]])
            if n == nb - 1:
                for half in range(2):
                    lo = 64 * half
                    ob = OB[half]
                    S.dve(lambda e, lo=lo, ob=ob: e.tensor_copy(out=bufB[lo:lo + 64, j, :], in_=bank(ob)[lo:lo + 64, :]), reads=[PB[ob]], writes=[BB[j]])

        if single:
            for g in range(ns):
                QK(g)
                E1(g)
                SP(g)
                TO(g)
                need = 0
                if feeder.gen is not None:
                    need = -(-feeder.left // (ns - g))
                    feeder.feed(min(need, SLOT_MM))
                E2(g)
                AV(g)
                if need > SLOT_MM and feeder.gen is not None:
                    feeder.feed(need - SLOT_MM)
            return
        QK(0)
        E1(0)
        SP(0)
        if ns > 1:
            QK(1)
        for g in range(ns):
            TO(g)
            if g + 1 < ns:
                E1(g + 1)
            need = 0
            if feeder is not None and feeder.gen is not None:
                need = -(-feeder.left // (ns - g))
                feeder.feed(min(need, SLOT_MM))
            E2(g)
            if g + 1 < ns:
                SP(g + 1)
            AV(g)
            if g + 2 < ns:
                QK(g + 2)
            if need > SLOT_MM and feeder.gen is not None:
                feeder.feed(need - SLOT_MM)

    def phase3a():
        fence(P2_BUFS, P1_BUFS)
        for pi in range(2):
            Wp, BWp = next_panel(P_WSB + pi)
            wv = Wp.rearrange("p (k c) -> p k c", k=8)
            for jj in range(4):
                j = pi * 4 + jj
                b = alloc_banks(1)
                fm_matmul(b, lambda k, jj=jj, wv=wv: wv[:, k, jj * 128:(jj + 1) * 128], lambda k: bufB[:, k, :], 8, C, BB + [BWp])
                i = nxt("p1")
                tmp = tmp_s[i][:, 0:C]
                S.dve(lambda e, b=b, j=j, tmp=tmp: e.tensor_tensor(out=tmp, in0=bank(b), in1=sg[:, 8 + j, :], op=ALU.mult),
                      reads=[PB[b], Bsg[8 + j]], writes=[Buu[i]])
                S.pool(lambda e, j=j, tmp=tmp: e.tensor_tensor(out=mT[:, j, :], in0=tmp, in1=sg[:, j, :], op=ALU.add),
                       reads=[Buu[i], Bsg[j]], writes=[Bm[j]])

    def dump(name, src, reads):
        if debug:
            S.dma("sync", lambda e: e.dma_start(out=dbg_d[name], in_=src), ds_dbg, reads=reads)

    xi = nxt("xn")
    S.dma("sync", lambda e: e.dma_start(out=Xn[xi][0:NMETA, :], in_=meta_d), ds_xn[xi], writes=[BXn[xi]])
    norm_transpose(Xn[xi][0:NMETA, :], [BXn[xi]], vecs[:, V_G1:V_G1 + 8], bufA, BA, 0, NMETA)
    phase1(NMETA, True, 0, BKm, 0, False)
    rot["bank"] = 0

    DB = [6, 7]
    pending = None
    p0_done = False
    for bseq in range(nseq):
        for c in range(nch):
            t0 = c * C
            if not p0_done:
                for tt in range(4):
                    xi = nxt("xn")
                    r0 = t0 + tt * 128
                    S.dma("sync", lambda e, bseq=bseq, r0=r0, xi=xi: e.dma_start(out=Xn[xi][:], in_=x_d[bseq, r0:r0 + 128, :]), ds_xn[xi], writes=[BXn[xi]])
                    norm_transpose(Xn[xi][:], [BXn[xi]], vecs[:, V_G1:V_G1 + 8], bufA, BA, tt * 128, 128)
            p0_done = False
            first = (bseq == 0 and c == 0)
            if debug and first:
                dump("d_nT", bufA[:], BA)
            phase1(C, False, NMETA + t0, [BKT[j][c] for j in range(8)], 4 * c, c == 0)
            rot["bank"] = 0
            if debug and first:
                dump("d_qT", qT[:], BqT)
                dump("d_uT", bufB[:], BB)
                dump("d_sg", sg[:], Bsg)
                dump("d_Vm", Vm[:], [BVm])
            feeder = None
            if pending is not None:
                ci = bseq * nch + c + 1
                nxc = (ci // nch, ci % nch) if ci < nseq * nch else None
                dbk = [2, 3, 6, 7] if c <= 1 else DB
                feeder = DenseFeeder(dense_items(pending[0], pending[1], dbk, nxc), N_DENSE_MM + (32 if nxc else 0), lag=max(1, len(dbk) - 2))
                p0_done = nxc is not None
            phase2(c, feeder)
            if feeder is not None:
                feeder.drain()
            rot["bank"] = 0
            if debug and first:
                dump("d_oT", bufB[:], BB)
            if debug and bseq == 0 and c == nch - 1:
                dump("d_KT", KT[:], BKm + [BKT[j][cc] for j in range(8) for cc in range(nch)])
                dump("d_V", Vc[:], BV)
            phase3a()
            rot["bank"] = 0
            S.dma("sync", lambda e, bseq=bseq, t0=t0: e.dma_start(out=H[:], in_=x_d[bseq, t0:t0 + C, :].rearrange("(t p) d -> p t d", p=128)),
                  ds_h, writes=BH)
            pending = (bseq, c)
    feeder = DenseFeeder(dense_items(pending[0], pending[1], [0, 1, 2, 3, 4, 5, 6, 7]), N_DENSE_MM)
    feeder.drain()
    assert st["next"] == len(stream), (st["next"], len(stream))
    S.finish()
    return nc, S


_CACHE = {}


def kernel(x, meta_tokens, norm1_g, w_in, b_gate, conv_w, w_conv_out, q_norm_g, k_norm_g, w_sb_out, w_o,
           norm2_g, w_ffn_in, w_ffn_out):
    x = np.ascontiguousarray(np.asarray(x, dtype=np.float32))
    f = lambda a: np.asarray(a, dtype=np.float32)
    pans = _host_panels(f(w_in)[0], f(w_conv_out)[0], f(w_sb_out)[0], f(w_o)[0], f(w_ffn_in)[0], f(w_ffn_out)[0])
    vecs = _host_vecs(f(norm1_g)[0], f(norm2_g)[0], f(b_gate)[0], f(conv_w)[0], f(q_norm_g)[0], f(k_norm_g)[0])
    consts = _host_consts()
    meta = np.ascontiguousarray(f(meta_tokens))
    if "nc" not in _CACHE:
        _CACHE["nc"] = build_program()[0]
    nc = _CACHE["nc"]
    ncores = 8
    in_maps = [{"x": x[i * NSEQ:(i + 1) * NSEQ], "meta": meta, "vecs": vecs, "consts": consts, "wpan": pans} for i in range(ncores)]
    res = run_bass_kernel_spmd(nc, in_maps, core_ids=list(range(ncores)))
    return np.concatenate([r["out"] for r in res.results], axis=0)
```

```python
from contextlib import ExitStack
import numpy as np
import concourse.bass as bass
import concourse.mybir as mybir
from concourse.bass_utils import run_bass_kernel_spmd

F32 = mybir.dt.float32
BF16 = mybir.dt.bfloat16
AF = mybir.ActivationFunctionType
ALU = mybir.AluOpType


class Buf:
    __slots__ = ("name", "w", "r")

    def __init__(self, name):
        self.name = name
        self.w = None
        self.r = []


class DSem:
    __slots__ = ("name", "sem", "count")

    def __init__(self, name):
        self.name = name
        self.sem = None
        self.count = 0


class Op:
    __slots__ = ("eng", "fn", "deps", "waits", "mark", "count", "dsem", "dval", "pos", "idx")


ENGS = ("pe", "act", "dve", "pool", "sp")
QUEUE_ENG = {"sync": "sp", "gpsimd": "pool", "scalar": "act"}


class Sched:
    def __init__(self, nc):
        self.nc = nc
        self.stack = ExitStack()
        self.ops = []
        self.eng_ops = {e: [] for e in ENGS}
        self.dsems = []

    def sbuf(self, name, shape, dtype):
        return self.stack.enter_context(self.nc.sbuf_tensor("sb_" + name, list(shape), dtype))

    def psum(self, name, shape, dtype):
        return self.stack.enter_context(self.nc.psum_tensor("ps_" + name, list(shape), dtype))

    def buf(self, name):
        return Buf(name)

    def dsem(self, name):
        d = DSem(name)
        self.dsems.append(d)
        return d

    def _add(self, eng, fn, reads, writes, dsem=None):
        op = Op()
        op.eng = eng
        op.fn = fn
        op.idx = len(self.ops)
        op.pos = len(self.eng_ops[eng])
        op.mark = False
        op.count = 0
        op.waits = []
        op.dsem = dsem
        op.dval = 0
        if dsem is not None:
            dsem.count += 16
            op.dval = dsem.count
        deps = []
        for b in reads:
            if b.w is not None:
                deps.append((b.w, True))
        for b in writes:
            if b.w is not None:
                deps.append((b.w, False))
            for r in b.r:
                if r is not op:
                    deps.append((r, False))
        for b in writes:
            b.w = op
            b.r = []
        for b in reads:
            if b.w is not op:
                b.r.append(op)
        op.deps = deps
        self.ops.append(op)
        self.eng_ops[eng].append(op)
        return op

    def pe(self, fn, reads=(), writes=()):
        return self._add("pe", fn, reads, writes)

    def act(self, fn, reads=(), writes=()):
        return self._add("act", fn, reads, writes)

    def dve(self, fn, reads=(), writes=()):
        return self._add("dve", fn, reads, writes)

    def pool(self, fn, reads=(), writes=()):
        return self._add("pool", fn, reads, writes)

    def dma(self, queue, fn, dsem, reads=(), writes=()):
        return self._add(QUEUE_ENG[queue], fn, reads, writes, dsem=dsem)

    def finish(self):
        nc = self.nc
        seen_pos = {e: {f: -1 for f in ENGS} for e in ENGS}
        seen_ds = {e: {} for e in ENGS}
        for op in self.ops:
            e = op.eng
            best = {}
            bestd = {}
            for (y, raw) in op.deps:
                if y.dsem is not None:
                    d = y.dsem
                    if seen_ds[e].get(d, 0) >= y.dval:
                        continue
                    if bestd.get(d, 0) < y.dval:
                        bestd[d] = y.dval
                    continue
                f = y.eng
                if f == e and op.dsem is None:
                    if e == "pe" or not raw:
                        continue
                if y.pos <= seen_pos[e][f]:
                    continue
                if f not in best or best[f].pos < y.pos:
                    best[f] = y
            for f, y in best.items():
                y.mark = True
                seen_pos[e][f] = y.pos
                op.waits.append(("c", y))
            for d, v in bestd.items():
                seen_ds[e][d] = v
                op.waits.append(("d", d, v))
        sems = {}
        for e in ENGS:
            sems[e] = self.stack.enter_context(nc.semaphore("sem_" + e))
            c = 0
            for op in self.eng_ops[e]:
                if op.mark:
                    c += 1
                    op.count = c
        for d in self.dsems:
            d.sem = self.stack.enter_context(nc.semaphore("ds_" + d.name))
        self.n_marks = {e: sum(1 for o in self.eng_ops[e] if o.mark) for e in ENGS}
        block = self.stack.enter_context(nc.Block())

        def emit(engname, eh):
            for op in self.eng_ops[engname]:
                for w in op.waits:
                    if w[0] == "c":
                        y = w[1]
                        eh.wait_ge(sems[y.eng], y.count)
                    else:
                        eh.wait_ge(w[1].sem, w[2])
                ins = op.fn(eh)
                if op.dsem is not None:
                    ins.then_inc(op.dsem.sem, 16)
                elif op.mark:
                    ins.then_inc(sems[engname], 1)
            if engname == "sp":
                for d in self.dsems:
                    if d.count > 0:
                        eh.wait_ge(d.sem, d.count)

        @block.tensor
        def _(eh):
            emit("pe", eh)

        @block.scalar
        def _(eh):
            emit("act", eh)

        @block.vector
        def _(eh):
            emit("dve", eh)

        @block.gpsimd
        def _(eh):
            emit("pool", eh)

        @block.sync
        def _(eh):
            emit("sp", eh)

        self.stack.close()


D = 1024
SEQ = 2048
NMETA = 16
NH = 16
DH = 64
DFF = 2816
NFF = DFF // 128
C = 512
NCH = SEQ // C
NSEQ = 4
EPS = 1e-6
NEG = -30000.0
PW = 4096
NPAN = 41
NW = 3
SLOT_MM = 3
P_G, P_CONV, P_WCO, P_Q, P_K, P_V, P_WSB, P_WO, P_FIN, P_FOUT = 0, 4, 12, 14, 16, 18, 20, 22, 24, 35
V_G1, V_G2, V_BG, V_CW, V_GQ, V_GK, NVEC = 0, 8, 16, 32, 56, 57, 58
C_ID, C_TRI, C_ONE, C_BLK, C_MSK, NCONST = 0, 128, 256, 384, 512, 512 + 4 * 512


def _host_consts():
    c = np.zeros((128, NCONST), np.float32)
    j = np.arange(128)[:, None]
    s = np.arange(128)[None, :]
    c[:, C_ID:C_ID + 128] = (j == s)
    c[:, C_TRI:C_TRI + 128] = -(j >= s).astype(np.float32)
    c[:, C_ONE:C_ONE + 128] = -1.0
    c[:, C_BLK:C_BLK + 128] = ((j // 64) == (s // 64))
    ql = np.arange(512)[None, :]
    for r in range(4):
        c[:, C_MSK + r * 512:C_MSK + (r + 1) * 512] = np.where(128 * r + j < ql, 0.0, NEG)
    return c


def _host_panels(w_in, w_conv_out, w_sb_out, w_o, w_ffn_in, w_ffn_out):
    pans = np.zeros((NPAN, 128, PW), np.float32)

    def colpanel(W, cols):
        sub = W[:, cols]
        n = sub.shape[1]
        return sub.reshape(8, 128, n).transpose(1, 0, 2).reshape(128, 8 * n)

    def put(i, a):
        pans[i, :, :a.shape[1]] = a

    ar = np.arange
    for pi in range(4):
        put(P_G + pi, colpanel(w_in, 6144 + pi * 512 + ar(512)))
    for j in range(8):
        cols = np.concatenate([j * 128 + ar(128), 1024 + j * 128 + ar(128), 2048 + j * 128 + ar(128)])
        put(P_CONV + j, colpanel(w_in, cols))
    for pi in range(2):
        put(P_WCO + pi, colpanel(w_conv_out, pi * 512 + ar(512)))
        put(P_Q + pi, colpanel(w_in, 3072 + pi * 512 + ar(512)))
        put(P_K + pi, colpanel(w_in, 4096 + pi * 512 + ar(512)))
        put(P_V + pi, colpanel(w_in, 5120 + pi * 512 + ar(512)))
        put(P_WSB + pi, colpanel(w_sb_out, pi * 512 + ar(512)))
        put(P_WO + pi, colpanel(w_o, pi * 512 + ar(512)))
    for i in range(11):
        cols = np.concatenate([(2 * i) * 128 + ar(128), DFF + (2 * i) * 128 + ar(128),
                               (2 * i + 1) * 128 + ar(128), DFF + (2 * i + 1) * 128 + ar(128)])
        put(P_FIN + i, colpanel(w_ffn_in, cols))
    for pi in range(6):
        nk = min(4, NFF - 4 * pi)
        blk = w_ffn_out[4 * pi * 128:(4 * pi + nk) * 128, :]
        a = blk.reshape(nk, 128, 2, 512).transpose(1, 0, 2, 3).reshape(128, nk * 2 * 512)
        put(P_FOUT + pi, a)
    return pans


def _host_vecs(norm1_g, norm2_g, b_gate, conv_w, q_norm_g, k_norm_g):
    v = np.zeros((128, NVEC), np.float32)
    v[:, V_G1:V_G1 + 8] = norm1_g.reshape(8, 128).T
    v[:, V_G2:V_G2 + 8] = norm2_g.reshape(8, 128).T
    v[:, V_BG:V_BG + 16] = b_gate.reshape(16, 128).T
    v[:, V_CW:V_CW + 24] = conv_w.reshape(3, 8, 128).transpose(2, 1, 0).reshape(128, 24)
    v[:, V_GQ] = np.tile(q_norm_g.reshape(64), 2)
    v[:, V_GK] = np.tile(k_norm_g.reshape(64), 2)
    return v


def build_program(nseq=NSEQ, nch=NCH, debug=False):
    nc = bass.Bass("TRN2", target_bir_lowering=False)
    x_d = nc.dram_tensor("x", [nseq, SEQ, D], F32, kind="ExternalInput").ap()
    meta_d = nc.dram_tensor("meta", [NMETA, D], F32, kind="ExternalInput").ap()
    vecs_d = nc.dram_tensor("vecs", [128, NVEC], F32, kind="ExternalInput").ap()
    consts_d = nc.dram_tensor("consts", [128, NCONST], F32, kind="ExternalInput").ap()
    wpan_d = nc.dram_tensor("wpan", [NPAN, 128, PW], F32, kind="ExternalInput").ap()
    out_d = nc.dram_tensor("out", [nseq, SEQ, D], F32, kind="ExternalOutput").ap()
    wscr_d = nc.dram_tensor("wscr", [NPAN, 128, PW], BF16, kind="Internal").ap()
    dbg_d = {}
    if debug:
        for nm, shp in (("d_nT", [128, 8, C]), ("d_qT", [128, 8, C]), ("d_KT", [128, 8, NMETA + SEQ]),
                        ("d_V", [128, 16, D]), ("d_uT", [128, 8, C]), ("d_sg", [128, 16, C]),
                        ("d_oT", [128, 8, C]), ("d_Vm", [NMETA, D])):
            dbg_d[nm] = nc.dram_tensor(nm, shp, BF16, kind="ExternalOutput").ap()

    S = Sched(nc)
    cst = S.sbuf("cst", [128, NCONST], BF16)
    vecs = S.sbuf("vecs", [128, NVEC], F32)
    Xn = [S.sbuf(f"Xn{i}", [128, D], F32) for i in range(2)]
    H = S.sbuf("H", [128, 4, D], F32)
    XS = [S.sbuf(f"XS{i}", [128, D], BF16) for i in range(2)]
    bufA = S.sbuf("bufA", [128, 8, C], BF16)
    mT = S.sbuf("mT", [128, 8, C], BF16)
    actS = [S.sbuf(f"actS{i}", [128, 4, C], BF16) for i in range(2)]
    bufB = S.sbuf("bufB", [128, 8, C], BF16)
    sg = S.sbuf("sg", [128, 16, C], BF16)
    qT = S.sbuf("qT", [128, 8, C], BF16)
    KT = S.sbuf("KT", [128, 8, NMETA + SEQ], BF16)
    Vc = S.sbuf("Vc", [128, 16, D], BF16)
    Vm = S.sbuf("Vm", [NMETA, D], BF16)
    U = S.sbuf("U", [128, 10240], BF16)
    W = [S.sbuf(f"W{i}", [128, PW], BF16) for i in range(NW)]
    zt = S.sbuf("zt", [128, C], BF16)
    stat = S.sbuf("stat", [128, 4, 4], F32)
    hs = S.sbuf("hs", [128, 8, 2], F32)
    hs0 = S.sbuf("hs0", [128, 8, 2], F32)
    PS = S.psum("PS", [128, 8 * 512], F32)

    ident = cst[:, C_ID:C_ID + 128]
    triN = cst[:, C_TRI:C_TRI + 128]
    onesN = cst[:, C_ONE:C_ONE + 128]
    blk = cst[:, C_BLK:C_BLK + 128]
    msk = cst[:, C_MSK:C_MSK + 2048].rearrange("p (r q) -> p r q", r=4)

    def bank(b, n=1):
        return PS[:, b * 512:(b + n) * 512]

    def bank_bf(b):
        return PS[:, b * 512:(b + 1) * 512].bitcast(BF16)

    def u_e(u):
        return U[:, u * 2048:(u + 1) * 2048].bitcast(F32)
    def u_sp(u):
        return U[:, 4096 + u * 1024:4096 + (u + 1) * 1024]
    def u_a(u):
        return U[:, 6144 + u * 1024:6144 + (u + 1) * 1024]
    def u_S(i):
        return U[:, 8192 + i * 1024:8192 + (i + 1) * 1024]
    def u_f32(slot):
        return U[:, slot * 1040:(slot + 1) * 1040].bitcast(F32)
    xc_s = [u_f32(0), u_f32(1)]
    uu_s = [u_f32(2), u_f32(3)]
    y_s = [u_f32(4), u_f32(5)]
    rs_s = [u_f32(6), u_f32(7)]
    sq_s = [U[:, 8 * 1040 + i * 512:8 * 1040 + (i + 1) * 512] for i in range(2)]
    tmp_s = [u_f32(2), u_f32(3)]
    sl_s = [u_f32(0), u_f32(1)]

    Bcst, Bvec = S.buf("cst"), S.buf("vecs")
    Bz = S.buf("zt")
    BXn = [S.buf("Xn0"), S.buf("Xn1")]
    BH = [S.buf(f"H{t}") for t in range(4)]
    BXS = [S.buf("XS0"), S.buf("XS1")]
    BA = [S.buf(f"A{k}") for k in range(8)]
    BB = [S.buf(f"B{k}") for k in range(8)]
    Bsg = [S.buf(f"sg{k}") for k in range(16)]
    BqT = [S.buf(f"qT{k}") for k in range(8)]
    BKm = [S.buf(f"KTm{k}") for k in range(8)]
    BKT = [[S.buf(f"KT{k}_{c}") for c in range(NCH)] for k in range(8)]
    BV = [S.buf(f"V{t}") for t in range(16)]
    BVm = S.buf("Vm")
    BW = [S.buf(f"W{i}") for i in range(NW)]
    Bm = [S.buf(f"m{k}") for k in range(8)]
    BactS = [S.buf("actS0"), S.buf("actS1")]
    Bstat = [S.buf(f"stat{i}") for i in range(4)]
    Bhs = [S.buf(f"hs{k}") for k in range(8)]
    Bhs0 = S.buf("hs0")
    PB = [S.buf(f"pb{i}") for i in range(8)]
    BU = S.buf("U")
    Be = [S.buf("e0"), S.buf("e1")]
    Bsp = [S.buf("sp0"), S.buf("sp1")]
    Ba = [S.buf("a0"), S.buf("a1")]
    BS = [S.buf("S0"), S.buf("S1")]
    Bxc = [S.buf("xc0"), S.buf("xc1")]
    Buu = [S.buf("uu0"), S.buf("uu1")]
    By = [S.buf("y0"), S.buf("y1")]
    Brs = [S.buf("rs0"), S.buf("rs1")]
    Bsq = [S.buf("sq0"), S.buf("sq1")]
    Bwscr = [S.buf(f"wscr{i}") for i in range(NPAN)]
    P1_BUFS = Bxc + Buu + By + Brs + Bsq
    P2_BUFS = Be + Bsp + Ba + BS

    def fence(olds, news):
        acc = []
        for o in olds:
            if o.w is not None:
                acc.append(o.w)
            acc.extend(o.r)
        for n in news:
            n.r = n.r + acc

    ds_c = S.dsem("c")
    ds_c2 = S.dsem("c2")
    ds_xn = [S.dsem("xn0"), S.dsem("xn1")]
    ds_h = S.dsem("h")
    ds_w = [S.dsem(f"w{i}") for i in range(NW)]
    ds_cv = [S.dsem("cv0"), S.dsem("cv1")]
    ds_scr = [S.dsem(f"scr{i}") for i in range(NW)]
    ds_out = [S.dsem(f"out{i}") for i in range(4)]
    ds_dbg = S.dsem("dbg")

    S.dma("sync", lambda e: e.dma_start(out=vecs[:], in_=vecs_d), ds_c, writes=[Bvec])
    cstage = H.rearrange("p a b -> p (a b)")[:, 0:NCONST]
    S.dma("sync", lambda e: e.dma_start(out=cstage, in_=consts_d), ds_c2, writes=BH)
    S.dve(lambda e: e.tensor_copy(out=cst[:], in_=cstage), reads=BH, writes=[Bcst])
    S.pool(lambda e: e.memset(hs0[:], 0.0), writes=[Bhs0])
    S.dve(lambda e: e.memset(zt[:], 0.0), writes=[Bz])

    stg = [H.rearrange("p a b -> p (a b)"), sg.rearrange("p a b -> p (a b)").bitcast(F32)]
    Bstg = [BH, Bsg]
    def cv_load(i):
        s2 = i % 2
        S.dma("sync", lambda e: e.dma_start(out=stg[s2], in_=wpan_d[i]), ds_cv[s2], writes=Bstg[s2])

    def cv_cast_store(i):
        s2 = i % 2
        s4 = i % NW
        if i % 3 == 0:
            S.dve(lambda e: e.tensor_copy(out=W[s4][:], in_=stg[s2]), reads=Bstg[s2], writes=[BW[s4]])
        else:
            S.act(lambda e: e.activation(out=W[s4][:], in_=stg[s2], func=AF.Copy), reads=Bstg[s2], writes=[BW[s4]])
        S.dma("sync", lambda e: e.dma_start(out=wscr_d[i], in_=W[s4][:]), ds_scr[s4], reads=[BW[s4]], writes=[Bwscr[i]])

    cv_load(0)
    for i in range(NPAN):
        if i + 1 < NPAN:
            cv_load(i + 1)
        cv_cast_store(i)

    serial_stream = (list(range(P_G, P_G + 4)) + list(range(P_CONV, P_CONV + 8)) + [P_WCO, P_WCO + 1]
                     + [P_Q, P_Q + 1, P_K, P_K + 1, P_V, P_V + 1])
    ffn_stream = []
    for gi in range(6):
        ffn_stream += [P_FIN + 2 * gi] + ([P_FIN + 2 * gi + 1] if 2 * gi + 1 < 11 else [])
        if gi >= 1:
            ffn_stream += [P_FOUT + gi - 1]
    ffn_stream += [P_FOUT + 5]
    dense_stream = [P_WO, P_WO + 1] + ffn_stream
    meta_stream = list(range(P_CONV, P_CONV + 8)) + [P_K, P_K + 1, P_V, P_V + 1]
    stream = list(meta_stream)
    for ci in range(nseq * nch):
        stream += serial_stream + (dense_stream if ci > 0 else []) + [P_WSB, P_WSB + 1]
    stream += dense_stream
    st = {"issued": 0, "next": 0}

    def issue_to(n):
        while st["issued"] < min(n, len(stream)):
            i = st["issued"]
            pid = stream[i]
            s4 = i % NW
            S.dma("sync", lambda e, pid=pid, s4=s4: e.dma_start(out=W[s4][:], in_=wscr_d[pid]), ds_w[s4],
                  reads=[Bwscr[pid]], writes=[BW[s4]])
            st["issued"] += 1

    def next_panel(expect):
        i = st["next"]
        assert stream[i] == expect, (i, stream[i], expect)
        issue_to(i + NW)
        st["next"] += 1
        s4 = i % NW
        return W[s4], BW[s4]

    rot = {"bank": 0, "xn": 0, "xs": 0, "stat": 0, "p1": 0}

    def alloc_banks(n):
        if rot["bank"] + n > 8:
            rot["bank"] = 0
        b = rot["bank"]
        rot["bank"] = (b + n) % 8
        return b

    def nxt(key, m=2):
        v = rot[key]
        rot[key] = (v + 1) % m
        return v

    def norm_transpose(src, src_bufs, gcol, dst, dst_bufs, off, rows):
        si = nxt("stat", 4)
        xi = nxt("xs")
        stt = stat[:, si, :]
        S.pool(lambda e: e.memset(stt[:rows, 0:1], 0.0), writes=[Bstat[si]])
        S.act(lambda e: e.activation(out=XS[xi][:rows, :], in_=src, func=AF.Square, accum_out=stt[:rows, 0:1]),
              reads=src_bufs + [Bstat[si]], writes=[BXS[xi], Bstat[si]])
        S.act(lambda e: e.activation(out=stt[:rows, 1:2], in_=stt[:rows, 0:1], func=AF.Ln, scale=1.0 / D, bias=EPS),
              reads=[Bstat[si]], writes=[Bstat[si]])
        S.act(lambda e: e.activation(out=stt[:rows, 2:3], in_=stt[:rows, 1:2], func=AF.Exp, scale=-0.5),
              reads=[Bstat[si]], writes=[Bstat[si]])
        S.dve(lambda e: e.tensor_scalar(out=XS[xi][:rows, :], in0=src, scalar1=stt[:rows, 2:3], scalar2=None, op0=ALU.mult),
              reads=src_bufs + [Bstat[si]], writes=[BXS[xi]])
        b = alloc_banks(1)
        pt = bank_bf(b)
        for k in range(8):
            S.pe(lambda e, k=k: e.transpose(out=pt[:, k * 128:k * 128 + rows], in_=XS[xi][:rows, k * 128:(k + 1) * 128],
                                            identity=ident[:rows, :rows]),
                 reads=[BXS[xi], Bcst], writes=[PB[b]])
        ptv = pt.rearrange("p (k t) -> p k t", k=8)[:, :, 0:rows]
        S.dve(lambda e: e.tensor_tensor(out=dst[:, :, off:off + rows], in0=ptv,
                                        in1=gcol.unsqueeze(2).to_broadcast([128, 8, rows]), op=ALU.mult),
              reads=[PB[b], Bvec], writes=dst_bufs)

    def fm_matmul(b, lhs_of_k, rhs_of_k, nk, ntok, reads):
        for k in range(nk):
            S.pe(lambda e, k=k: e.matmul(bank(b)[:, 0:ntok], lhsT=lhs_of_k(k), rhs=rhs_of_k(k), start=(k == 0), stop=(k == nk - 1)),
                 reads=reads, writes=[PB[b]])

    hn_pending = []

    def head_norm(b, gcolumn, lnbias, dst, dst_bufs, ntok):
        i = nxt("p1")
        sq, rs = sq_s[i][:, 0:ntok], rs_s[i][:, 0:ntok]
        S.act(lambda e: e.activation(out=sq, in_=bank(b)[:, 0:ntok], func=AF.Square), reads=[PB[b]], writes=[Bsq[i]])
        b2 = alloc_banks(1)
        S.pe(lambda e: e.matmul(bank(b2)[:, 0:ntok], lhsT=blk, rhs=sq, start=True, stop=True), reads=[Bsq[i], Bcst], writes=[PB[b2]])

        def stage_b():
            S.act(lambda e: e.activation(out=rs, in_=bank(b2)[:, 0:ntok], func=AF.Ln, scale=1.0 / DH, bias=EPS), reads=[PB[b2]], writes=[Brs[i]])
            S.act(lambda e: e.activation(out=rs, in_=rs, func=AF.Exp, scale=-0.5, bias=lnbias), reads=[Brs[i]], writes=[Brs[i]])
            S.dve(lambda e: e.scalar_tensor_tensor(out=dst, in0=bank(b)[:, 0:ntok], scalar=gcolumn, in1=rs, op0=ALU.mult, op1=ALU.mult),
                  reads=[PB[b], Brs[i], Bvec], writes=dst_bufs)
        hn_flush()
        hn_pending.append(stage_b)

    def hn_flush():
        while hn_pending:
            hn_pending.pop(0)()

    def phase1(ntok, meta, kt_off, kt_bufs, vtile0, seq_first):
        nT = bufA
        rhsA = lambda k: nT[:, k, 0:ntok]
        if not meta:
            for pi in range(4):
                Wp, BWp = next_panel(P_G + pi)
                wv = Wp.rearrange("p (k c) -> p k c", k=8)
                for jj in range(4):
                    j = pi * 4 + jj
                    b = alloc_banks(1)
                    fm_matmul(b, lambda k, jj=jj, wv=wv: wv[:, k, jj * 128:(jj + 1) * 128], rhsA, 8, ntok, BA + [BWp])
                    S.act(lambda e, b=b, j=j: e.activation(out=sg[:, j, :], in_=bank(b), func=AF.Sigmoid, bias=vecs[:, V_BG + j:V_BG + j + 1]),
                          reads=[PB[b], Bvec], writes=[Bsg[j]])
        for j in range(8):
            Wp, BWp = next_panel(P_CONV + j)
            wv = Wp[:, 0:8 * 384].rearrange("p (k c) -> p k c", k=8)
            b0 = alloc_banks(3)
            tiles = (1, 2) if meta else (0, 1, 2)
            for t in tiles:
                fm_matmul(b0 + t, lambda k, t=t, wv=wv: wv[:, k, t * 128:(t + 1) * 128], rhsA, 8, ntok, BA + [BWp])
            i = nxt("p1")
            xc, uu, y = xc_s[i], uu_s[i], y_s[i]
            S.act(lambda e, b0=b0, xc=xc: e.activation(out=xc[:, 0:ntok], in_=bank(b0 + 1)[:, 0:ntok], func=AF.Copy), reads=[PB[b0 + 1]], writes=[Bxc[i]])
            if not meta:
                src_h, src_b = (hs0, Bhs0) if seq_first else (hs, Bhs[j])
                S.pool(lambda e, uu=uu, src_h=src_h, j=j: e.tensor_copy(out=uu[:, 0:2], in_=src_h[:, j, :]), reads=[src_b], writes=[Buu[i]])
            S.dve(lambda e, b0=b0, xc=xc, uu=uu: e.tensor_tensor(out=uu[:, 2:2 + ntok], in0=bank(b0 + 2)[:, 0:ntok], in1=xc[:, 0:ntok], op=ALU.mult),
                  reads=[PB[b0 + 2], Bxc[i]], writes=[Buu[i]])
            dst_h, dst_b = (hs0, Bhs0) if meta else (hs, Bhs[j])
            S.pool(lambda e, uu=uu, dst_h=dst_h, j=j: e.tensor_copy(out=dst_h[:, j, :], in_=uu[:, ntok:ntok + 2]), reads=[Buu[i]], writes=[dst_b])
            if meta:
                continue
            cw = lambda tap, j=j: vecs[:, V_CW + j * 3 + tap:V_CW + j * 3 + tap + 1]
            S.dve(lambda e, uu=uu, y=y, cw=cw: e.tensor_scalar(out=y[:, 0:ntok], in0=uu[:, 2:2 + ntok], scalar1=cw(2), scalar2=None, op0=ALU.mult),
                   reads=[Buu[i], Bvec], writes=[By[i]])
            S.dve(lambda e, uu=uu, y=y, cw=cw: e.scalar_tensor_tensor(out=y[:, 0:ntok], in0=uu[:, 1:1 + ntok], scalar=cw(1), in1=y[:, 0:ntok], op0=ALU.mult, op1=ALU.add),
                   reads=[Buu[i], By[i], Bvec], writes=[By[i]])
            S.dve(lambda e, uu=uu, y=y, cw=cw: e.scalar_tensor_tensor(out=y[:, 0:ntok], in0=uu[:, 0:ntok], scalar=cw(0), in1=y[:, 0:ntok], op0=ALU.mult, op1=ALU.add),
                   reads=[Buu[i], By[i], Bvec], writes=[By[i]])
            S.dve(lambda e, b0=b0, y=y, j=j: e.tensor_tensor(out=bufB[:, j, 0:ntok], in0=bank(b0)[:, 0:ntok], in1=y[:, 0:ntok], op=ALU.mult),
                  reads=[PB[b0], By[i]], writes=[BB[j]])
        if not meta:
            for pi in range(2):
                Wp, BWp = next_panel(P_WCO + pi)
                wv = Wp.rearrange("p (k c) -> p k c", k=8)
                for jj in range(4):
                    j = pi * 4 + jj
                    b = alloc_banks(1)
                    fm_matmul(b, lambda k, jj=jj, wv=wv: wv[:, k, jj * 128:(jj + 1) * 128], lambda k: bufB[:, k, 0:ntok], 8, ntok, BB + [BWp])
                    S.dve(lambda e, b=b, j=j: e.tensor_tensor(out=sg[:, j, :], in0=bank(b), in1=sg[:, j, :], op=ALU.mult),
                          reads=[PB[b], Bsg[j]], writes=[Bsg[j]])
            for pi in range(2):
                Wp, BWp = next_panel(P_Q + pi)
                wv = Wp.rearrange("p (k c) -> p k c", k=8)
                for jj in range(4):
                    j = pi * 4 + jj
                    b = alloc_banks(1)
                    fm_matmul(b, lambda k, jj=jj, wv=wv: wv[:, k, jj * 128:(jj + 1) * 128], rhsA, 8, ntok, BA + [BWp])
                    head_norm(b, vecs[:, V_GQ:V_GQ + 1], float(np.log(DH ** -0.5)), qT[:, j, 0:ntok], [BqT[j]], ntok)
        for pi in range(2):
            Wp, BWp = next_panel(P_K + pi)
            wv = Wp.rearrange("p (k c) -> p k c", k=8)
            for jj in range(4):
                j = pi * 4 + jj
                b = alloc_banks(1)
                fm_matmul(b, lambda k, jj=jj, wv=wv: wv[:, k, jj * 128:(jj + 1) * 128], rhsA, 8, ntok, BA + [BWp])
                head_norm(b, vecs[:, V_GK:V_GK + 1], 0.0, KT[:, j, kt_off:kt_off + ntok], [kt_bufs[j]], ntok)
        hn_flush()
        ntt = 1 if meta else ntok // 128
        rows = ntok if meta else 128
        vb = alloc_banks(8) if not meta else alloc_banks(2)
        for pi in range(2):
            Wp, BWp = next_panel(P_V + pi)
            wv = Wp.rearrange("p (k c) -> p k c", k=8)
            for tt in range(ntt):
                b = vb + tt * 2 + pi
                for k in range(8):
                    S.pe(lambda e, b=b, k=k, tt=tt, wv=wv: e.matmul(bank(b)[0:rows, :], lhsT=nT[:, k, tt * 128:tt * 128 + rows], rhs=wv[:, k, :],
                                                                  start=(k == 0), stop=(k == 7)),
                         reads=BA + [BWp], writes=[PB[b]])
                if meta:
                    S.dve(lambda e, b=b, pi=pi: e.tensor_copy(out=Vm[:, pi * 512:(pi + 1) * 512], in_=bank(b)[0:rows, :]), reads=[PB[b]], writes=[BVm])
                elif (tt + pi) % 2 == 0:
                    S.dve(lambda e, b=b, pi=pi, tt=tt: e.tensor_copy(out=Vc[:, vtile0 + tt, pi * 512:(pi + 1) * 512], in_=bank(b)), reads=[PB[b]], writes=[BV[vtile0 + tt]])
                else:
                    S.act(lambda e, b=b, pi=pi, tt=tt: e.activation(out=Vc[:, vtile0 + tt, pi * 512:(pi + 1) * 512], in_=bank(b), func=AF.Copy), reads=[PB[b]], writes=[BV[vtile0 + tt]])
    def dense_items(bseq, c, dbanks, nxt_chunk=None):
        rr = {"i": 0}

        def nb():
            b = dbanks[rr["i"] % len(dbanks)]
            rr["i"] += 1
            return b

        for pi in range(2):
            hold = {}
            for tt in range(4):
                b = nb()

                def mm(k, b=b, pi=pi, tt=tt, hold=hold):
                    if "w" not in hold:
                        hold["w"] = next_panel(P_WO + pi)
                    Wp, BWp = hold["w"]
                    wv = Wp.rearrange("p (k c) -> p k c", k=8)
                    S.pe(lambda e: e.matmul(bank(b), lhsT=mT[:, k, tt * 128:(tt + 1) * 128], rhs=wv[:, k, :], start=(k == 0), stop=(k == 7)),
                         reads=Bm + [BWp], writes=[PB[b]])
                pe = [(lambda k=k, mm=mm: mm(k)) for k in range(8)]

                def post(b=b, pi=pi, tt=tt):
                    S.dve(lambda e: e.tensor_tensor(out=H[:, tt, pi * 512:(pi + 1) * 512], in0=bank(b), in1=H[:, tt, pi * 512:(pi + 1) * 512], op=ALU.add),
                          reads=[PB[b], BH[tt]], writes=[BH[tt]])
                yield dict(mms=pe, post=post, bank=b, flush=False, pre=None)
        def norm_items(src_of, src_bufs_of, gcol, dst, dst_bufs, loader=None):
            holds = [dict() for _ in range(4)]

            def make_a(tt):
                def pre():
                    if loader is not None:
                        loader(tt)
                    si = nxt("stat", 4)
                    xs = nxt("xs")
                    holds[tt]["xs"] = xs
                    stt = stat[:, si, :]
                    src = src_of(tt)
                    sb = src_bufs_of(tt)
                    S.dve(lambda e: e.memset(stt[:, 0:1], 0.0), writes=[Bstat[si]])
                    S.act(lambda e: e.activation(out=XS[xs][:], in_=src, func=AF.Square, accum_out=stt[:, 0:1]),
                          reads=sb + [Bstat[si]], writes=[BXS[xs], Bstat[si]])
                    S.act(lambda e: e.activation(out=stt[:, 1:2], in_=stt[:, 0:1], func=AF.Ln, scale=1.0 / D, bias=EPS), reads=[Bstat[si]], writes=[Bstat[si]])
                    S.act(lambda e: e.activation(out=stt[:, 2:3], in_=stt[:, 1:2], func=AF.Exp, scale=-0.5), reads=[Bstat[si]], writes=[Bstat[si]])
                    S.dve(lambda e: e.tensor_scalar(out=XS[xs][:], in0=src, scalar1=stt[:, 2:3], scalar2=None, op0=ALU.mult),
                          reads=sb + [Bstat[si]], writes=[BXS[xs]])
                return dict(mms=[], post=None, bank=None, flush=(tt == 0), pre=pre)

            def make_b(tt):
                b = nb()

                def mm(k):
                    xs = holds[tt]["xs"]
                    pt = bank_bf(b)
                    S.pe(lambda e: e.transpose(out=pt[:, k * 128:(k + 1) * 128], in_=XS[xs][:, k * 128:(k + 1) * 128], identity=ident),
                         reads=[BXS[xs], Bcst], writes=[PB[b]])

                def post():
                    ptv = bank_bf(b).rearrange("p (k t) -> p k t", k=8)
                    S.dve(lambda e: e.tensor_tensor(out=dst[:, :, tt * 128:(tt + 1) * 128], in0=ptv,
                                                    in1=gcol.unsqueeze(2).to_broadcast([128, 8, 128]), op=ALU.mult),
                          reads=[PB[b], Bvec], writes=dst_bufs)
                return dict(mms=[(lambda k=k: mm(k)) for k in range(8)], post=post, bank=b, flush=False, pre=None)

            for kind, tt in (("A", 0), ("A", 1), ("B", 0), ("A", 2), ("B", 1), ("A", 3), ("B", 2), ("B", 3)):
                yield make_a(tt) if kind == "A" else make_b(tt)

        yield from norm_items(lambda tt: H[:, tt, :], lambda tt: [BH[tt]], vecs[:, V_G2:V_G2 + 8], mT, Bm)
        def ffn_in_items(gi):
            nk = min(4, NFF - 4 * gi)
            sl = gi % 2
            hold = {}
            for jj in range(nk):
                for t in range(2):
                    b = nb()

                    def mm(k, b=b, jj=jj, t=t, gi=gi, hold=hold):
                        if jj % 2 == 0 and t == 0 and k == 0:
                            hold["w"] = next_panel(P_FIN + 2 * gi + jj // 2)
                        Wp, BWp = hold["w"]
                        wv = Wp.rearrange("p (k c) -> p k c", k=8)
                        col = ((jj % 2) * 2 + t) * 128
                        S.pe(lambda e: e.matmul(bank(b), lhsT=wv[:, k, col:col + 128], rhs=mT[:, k, :], start=(k == 0), stop=(k == 7)),
                             reads=Bm + [BWp], writes=[PB[b]])
                    pe = [(lambda k=k, mm=mm: mm(k)) for k in range(8)]

                    def post(b=b, jj=jj, t=t, sl=sl, hold=hold):
                        if t == 0:
                            xi = nxt("xn")
                            hold["xi"] = xi
                            tmp = Xn[xi][:, 0:C]
                            S.act(lambda e: e.activation(out=tmp, in_=bank(b), func=AF.Exp, scale=-1.0), reads=[PB[b]], writes=[BXn[xi]])
                            S.act(lambda e: e.activation(out=tmp, in_=tmp, func=AF.Ln, bias=1.0), reads=[BXn[xi]], writes=[BXn[xi]])
                            S.act(lambda e: e.activation(out=tmp, in_=tmp, func=AF.Exp, scale=-1.0), reads=[BXn[xi]], writes=[BXn[xi]])
                            S.dve(lambda e: e.tensor_tensor(out=tmp, in0=bank(b), in1=tmp, op=ALU.mult), reads=[PB[b], BXn[xi]], writes=[BXn[xi]])
                        else:
                            xi = hold["xi"]
                            tmp = Xn[xi][:, 0:C]
                            S.dve(lambda e: e.tensor_tensor(out=actS[sl][:, jj, :], in0=bank(b), in1=tmp, op=ALU.mult),
                                  reads=[PB[b], BXn[xi]], writes=[BactS[sl]])
                    yield dict(mms=pe, post=post, bank=b, flush=(gi == 0 and jj == 0 and t == 0), pre=None)
        def fout_items(gi):
            nk = min(4, NFF - 4 * gi)
            sl = gi % 2
            hold2 = {}
            for tt in range(4):
                for half in range(2):
                    b = nb()

                    def mm(kk, b=b, tt=tt, half=half, gi=gi, nk=nk, sl=sl, hold2=hold2):
                        if "w" not in hold2:
                            hold2["w"] = next_panel(P_FOUT + gi)
                        Wp, BWp = hold2["w"]
                        wv = Wp[:, 0:nk * 1024].rearrange("p (s c) -> p s c", c=512)
                        S.pe(lambda e: e.matmul(bank(b), lhsT=actS[sl][:, kk, tt * 128:(tt + 1) * 128], rhs=wv[:, kk * 2 + half, :],
                                                start=(kk == 0), stop=(kk == nk - 1)),
                             reads=[BactS[sl], BWp], writes=[PB[b]])
                    pe = [(lambda kk=kk, mm=mm: mm(kk)) for kk in range(nk)]

                    def post(b=b, tt=tt, half=half, gi=gi):
                        S.dve(lambda e: e.tensor_tensor(out=H[:, tt, half * 512:(half + 1) * 512], in0=bank(b), in1=H[:, tt, half * 512:(half + 1) * 512], op=ALU.add),
                              reads=[PB[b], BH[tt]], writes=[BH[tt]])
                        if gi == 5 and half == 1:
                            r0 = c * C + tt * 128
                            S.dma("sync", lambda e: e.dma_start(out=out_d[bseq, r0:r0 + 128, :], in_=H[:, tt, :]), ds_out[tt], reads=[BH[tt]])
                    yield dict(mms=pe, post=post, bank=b, flush=(tt == 0 and half == 0), pre=None)

        for gi in range(6):
            yield from ffn_in_items(gi)
            if gi >= 1:
                yield from fout_items(gi - 1)
        yield from fout_items(5)

        if nxt_chunk is not None:
            nb_, nc_ = nxt_chunk
            xslot = {}

            def loader(tt):
                xi = nxt("xn")
                xslot[tt] = xi
                r0 = nc_ * C + tt * 128
                S.dma("sync", lambda e: e.dma_start(out=Xn[xi][:], in_=x_d[nb_, r0:r0 + 128, :]), ds_xn[xi], writes=[BXn[xi]])
            yield from norm_items(lambda tt: Xn[xslot[tt]][:], lambda tt: [BXn[xslot[tt]]], vecs[:, V_G1:V_G1 + 8], bufA, BA, loader)

    class DenseFeeder:
        def __init__(self, gen, n_mm):
            self.gen = gen
            self.left = n_mm
            self.pending = {}
            self.cur = None
            self.idx = 0

        def feed(self, n):
            while n > 0 and self.gen is not None:
                if self.cur is None:
                    try:
                        it = next(self.gen)
                    except StopIteration:
                        self.gen = None
                        self.flush()
                        return
                    if it["flush"]:
                        self.flush()
                    if it["bank"] is not None and it["bank"] in self.pending:
                        self.pending.pop(it["bank"])()
                    if it["pre"] is not None:
                        it["pre"]()
                    self.cur = it
                    self.idx = 0
                mms = self.cur["mms"]
                take = min(n, len(mms) - self.idx)
                for f in mms[self.idx:self.idx + take]:
                    f()
                self.idx += take
                n -= take
                self.left -= take
                if self.idx == len(mms):
                    if self.cur["post"] is not None:
                        for b in list(self.pending):
                            self.pending.pop(b)()
                        self.pending[self.cur["bank"]] = self.cur["post"]
                    self.cur = None

        def flush(self):
            for b in list(self.pending):
                self.pending.pop(b)()

        def drain(self):
            while self.gen is not None:
                self.feed(64)
            self.flush()

    N_DENSE_MM = 64 + 32 + 352 + 176

    def phase2(c, feeder):
        fence(P1_BUFS, P2_BUFS)
        blocks = [("d", 4 * c + r, r) for r in (3, 2, 1, 0)] + [("f", kt, 0) for kt in range(4 * c - 1, -1, -1)] + [("m", 0, 0)]
        nb = len(blocks)
        steps = [(j, n) for j in range(8) for n in range(nb)]
        ns = len(steps)
        ZD = [0, 2]
        OB = [4, 5]
        warm_fill = (feeder is not None and c >= 2)

        def geom(n):
            kind, kt, r = blocks[n]
            if kind == "m":
                return kind, kt, r, NMETA, 0, 0
            return kind, kt, r, 128, NMETA + kt * 128, (128 * r if kind == "d" else 0)

        def v3(ap2):
            return ap2.rearrange("p (h q) -> p h q", h=2)

        def QK(g):
            j, n = steps[g]
            kind, kt, r, nk, koff, q0 = geom(n)
            z = ZD[g % 2]
            kb = [BKm[j]] if kind == "m" else [BKT[j][kt // 4]]
            for half in range(2):
                lo = 64 * half
                b = z + half
                S.pe(lambda e, b=b, lo=lo: e.matmul(bank(b)[0:nk, q0:C], lhsT=KT[lo:lo + 64, j, koff:koff + nk], rhs=qT[lo:lo + 64, j, q0:C],
                                                    start=True, stop=(kind != "d")),
                     reads=kb + [BqT[j]], writes=[PB[b]])
            if kind == "d":
                for half in range(2):
                    b = z + half
                    S.pe(lambda e, b=b: e.matmul(bank(b)[:, q0:q0 + 128], lhsT=ident, rhs=msk[:, 0, 0:128], start=False, stop=True, skip_group_check=True),
                         reads=[Bcst], writes=[PB[b]])

        def E1(g):
            j, n = steps[g]
            nk, q0 = geom(n)[3], geom(n)[5]
            u = g % 2
            zd = v3(bank(ZD[u], 2))
            S.act(lambda e: e.activation(out=v3(u_e(u))[0:nk, :, q0:C], in_=zd[0:nk, :, q0:C], func=AF.Exp), reads=[PB[ZD[u]], PB[ZD[u] + 1]], writes=[Be[u]])

        def SP(g):
            j, n = steps[g]
            nk, q0 = geom(n)[3], geom(n)[5]
            u = g % 2
            S.act(lambda e: e.activation(out=v3(u_sp(u))[0:nk, :, q0:C], in_=v3(u_e(u))[0:nk, :, q0:C], func=AF.Ln, bias=1.0), reads=[Be[u]], writes=[Bsp[u]])

        def TO(g):
            j, n = steps[g]
            nk, q0 = geom(n)[3], geom(n)[5]
            u = g % 2
            si = n % 2
            if n == 0:
                for i in range(2):
                    S.dve(lambda e, i=i: e.memset(u_S(i), 0.0), writes=[BS[i]])
            for half in range(2):
                b = ZD[u] + half
                S.pe(lambda e, b=b, half=half: e.matmul(bank(b)[0:nk, q0:C], lhsT=triN[0:nk, 0:nk], rhs=v3(u_sp(u))[0:nk, half, q0:C],
                                                       start=False, stop=(n == 0), skip_group_check=True),
                     reads=[Bsp[u], Bcst], writes=[PB[b]])
                if n > 0:
                    S.pe(lambda e, b=b, half=half: e.matmul(bank(b)[0:nk, q0:C], lhsT=onesN[:, 0:nk], rhs=v3(u_S(si))[:, half, q0:C],
                                                           start=False, stop=True, skip_group_check=True),
                         reads=[BS[si], Bcst], writes=[PB[b]])
            if n + 1 < nb:
                if n == 0:
                    S.dve(lambda e: e.tensor_copy(out=v3(u_S(1))[:, :, q0:C], in_=v3(u_sp(u))[:, :, q0:C]), reads=[Bsp[u]], writes=[BS[1]])
                else:
                    S.dve(lambda e: e.tensor_tensor(out=v3(u_S(1 - si))[:, :, q0:C], in0=v3(u_S(si))[:, :, q0:C], in1=v3(u_sp(u))[:, :, q0:C], op=ALU.add),
                          reads=[BS[si], Bsp[u]], writes=[BS[1 - si]])

        def E2(g):
            j, n = steps[g]
            nk, q0 = geom(n)[3], geom(n)[5]
            u = g % 2
            zd = v3(bank(ZD[u], 2))
            S.act(lambda e: e.activation(out=v3(u_a(u))[0:nk, :, q0:C], in_=zd[0:nk, :, q0:C], func=AF.Exp), reads=[PB[ZD[u]], PB[ZD[u] + 1]], writes=[Ba[u]])

        def AV(g):
            j, n = steps[g]
            kind, kt, r, nk, koff, q0 = geom(n)
            u = g % 2
            if kind == "m":
                lhs, rb = Vm[:, j * 128:(j + 1) * 128], [BVm]
            else:
                lhs, rb = Vc[:, kt, j * 128:(j + 1) * 128], [BV[kt]]
            for half in range(2):
                ob = OB[half]
                S.pe(lambda e, half=half, ob=ob: e.matmul(bank(ob)[:, q0:C], lhsT=lhs, rhs=v3(u_a(u))[0:nk, half, q0:C],
                                                         start=(n == 0), stop=(n == nb - 1), skip_group_check=True),
                     reads=rb + [Ba[u]], writes=[PB[ob]])
            if n == nb - 1:
                for half in range(2):
                    lo = 64 * half
                    ob = OB[half]
                    S.dve(lambda e, lo=lo, ob=ob: e.tensor_copy(out=bufB[lo:lo + 64, j, :], in_=bank(ob)[lo:lo + 64, :]), reads=[PB[ob]], writes=[BB[j]])

        QK(0)
        E1(0)
        SP(0)
        if ns > 1:
            QK(1)
        for g in range(ns):
            TO(g)
            if g + 1 < ns:
                E1(g + 1)
            need = 0
            if feeder is not None and feeder.gen is not None:
                need = -(-feeder.left // (ns - g))
                feeder.feed(min(need, SLOT_MM))
            E2(g)
            if g + 1 < ns:
                SP(g + 1)
            if warm_fill:
                for half in range(2):
                    ob = OB[half]
                    S.pe(lambda e, ob=ob: e.matmul(bank(ob), lhsT=ident, rhs=zt[:], start=False, stop=False, skip_group_check=True),
                         reads=[Bcst, Bz], writes=[PB[ob]])
            AV(g)
            if g + 2 < ns:
                QK(g + 2)
            if need > SLOT_MM and feeder.gen is not None:
                feeder.feed(need - SLOT_MM)

    def phase3a():
        fence(P2_BUFS, P1_BUFS)
        for pi in range(2):
            Wp, BWp = next_panel(P_WSB + pi)
            wv = Wp.rearrange("p (k c) -> p k c", k=8)
            for jj in range(4):
                j = pi * 4 + jj
                b = alloc_banks(1)
                fm_matmul(b, lambda k, jj=jj, wv=wv: wv[:, k, jj * 128:(jj + 1) * 128], lambda k: bufB[:, k, :], 8, C, BB + [BWp])
                i = nxt("p1")
                tmp = tmp_s[i][:, 0:C]
                S.dve(lambda e, b=b, j=j, tmp=tmp: e.tensor_tensor(out=tmp, in0=bank(b), in1=sg[:, 8 + j, :], op=ALU.mult),
                      reads=[PB[b], Bsg[8 + j]], writes=[Buu[i]])
                S.pool(lambda e, j=j, tmp=tmp: e.tensor_tensor(out=mT[:, j, :], in0=tmp, in1=sg[:, j, :], op=ALU.add),
                       reads=[Buu[i], Bsg[j]], writes=[Bm[j]])

    def dump(name, src, reads):
        if debug:
            S.dma("sync", lambda e: e.dma_start(out=dbg_d[name], in_=src), ds_dbg, reads=reads)

    xi = nxt("xn")
    S.dma("sync", lambda e: e.dma_start(out=Xn[xi][0:NMETA, :], in_=meta_d), ds_xn[xi], writes=[BXn[xi]])
    norm_transpose(Xn[xi][0:NMETA, :], [BXn[xi]], vecs[:, V_G1:V_G1 + 8], bufA, BA, 0, NMETA)
    phase1(NMETA, True, 0, BKm, 0, False)
    rot["bank"] = 0

    DB = [6, 7]
    pending = None
    p0_done = False
    for bseq in range(nseq):
        for c in range(nch):
            t0 = c * C
            if not p0_done:
                for tt in range(4):
                    xi = nxt("xn")
                    r0 = t0 + tt * 128
                    S.dma("sync", lambda e, bseq=bseq, r0=r0, xi=xi: e.dma_start(out=Xn[xi][:], in_=x_d[bseq, r0:r0 + 128, :]), ds_xn[xi], writes=[BXn[xi]])
                    norm_transpose(Xn[xi][:], [BXn[xi]], vecs[:, V_G1:V_G1 + 8], bufA, BA, tt * 128, 128)
            p0_done = False
            first = (bseq == 0 and c == 0)
            if debug and first:
                dump("d_nT", bufA[:], BA)
            phase1(C, False, NMETA + t0, [BKT[j][c] for j in range(8)], 4 * c, c == 0)
            rot["bank"] = 0
            if debug and first:
                dump("d_qT", qT[:], BqT)
                dump("d_uT", bufB[:], BB)
                dump("d_sg", sg[:], Bsg)
                dump("d_Vm", Vm[:], [BVm])
            feeder = None
            if pending is not None:
                ci = bseq * nch + c + 1
                nxc = (ci // nch, ci % nch) if ci < nseq * nch else None
                feeder = DenseFeeder(dense_items(pending[0], pending[1], DB, nxc), N_DENSE_MM + (32 if nxc else 0))
                p0_done = nxc is not None
            phase2(c, feeder)
            if feeder is not None:
                feeder.drain()
            rot["bank"] = 0
            if debug and first:
                dump("d_oT", bufB[:], BB)
            if debug and bseq == 0 and c == nch - 1:
                dump("d_KT", KT[:], BKm + [BKT[j][cc] for j in range(8) for cc in range(nch)])
                dump("d_V", Vc[:], BV)
            phase3a()
            rot["bank"] = 0
            S.dma("sync", lambda e, bseq=bseq, t0=t0: e.dma_start(out=H[:], in_=x_d[bseq, t0:t0 + C, :].rearrange("(t p) d -> p t d", p=128)),
                  ds_h, writes=BH)
            pending = (bseq, c)
    feeder = DenseFeeder(dense_items(pending[0], pending[1], [0, 1, 2, 3, 4, 5, 6, 7]), N_DENSE_MM)
    feeder.drain()
    assert st["next"] == len(stream), (st["next"], len(stream))
    S.finish()
    return nc, S


_CACHE = {}


def kernel(x, meta_tokens, norm1_g, w_in, b_gate, conv_w, w_conv_out, q_norm_g, k_norm_g, w_sb_out, w_o,
           norm2_g, w_ffn_in, w_ffn_out):
    x = np.ascontiguousarray(np.asarray(x, dtype=np.float32))
    f = lambda a: np.asarray(a, dtype=np.float32)
    pans = _host_panels(f(w_in)[0], f(w_conv_out)[0], f(w_sb_out)[0], f(w_o)[0], f(w_ffn_in)[0], f(w_ffn_out)[0])
    vecs = _host_vecs(f(norm1_g)[0], f(norm2_g)[0], f(b_gate)[0], f(conv_w)[0], f(q_norm_g)[0], f(k_norm_g)[0])
    consts = _host_consts()
    meta = np.ascontiguousarray(f(meta_tokens))
    if "nc" not in _CACHE:
        _CACHE["nc"] = build_program()[0]
    nc = _CACHE["nc"]
    ncores = 8
    in_maps = [{"x": x[i * NSEQ:(i + 1) * NSEQ], "meta": meta, "vecs": vecs, "consts": consts, "wpan": pans} for i in range(ncores)]
    res = run_bass_kernel_spmd(nc, in_maps, core_ids=list(range(ncores)))
    return np.concatenate([r["out"] for r in res.results], axis=0)
```

```python
from contextlib import ExitStack
import numpy as np
import concourse.bass as bass
import concourse.mybir as mybir
from concourse.bass_utils import run_bass_kernel_spmd

F32 = mybir.dt.float32
BF16 = mybir.dt.bfloat16
AF = mybir.ActivationFunctionType
ALU = mybir.AluOpType


class Buf:
    __slots__ = ("name", "w", "r")

    def __init__(self, name):
        self.name = name
        self.w = None
        self.r = []


class DSem:
    __slots__ = ("name", "sem", "count")

    def __init__(self, name):
        self.name = name
        self.sem = None
        self.count = 0


class Op:
    __slots__ = ("eng", "fn", "deps", "waits", "mark", "count", "dsem", "dval", "pos", "idx")


ENGS = ("pe", "act", "dve", "pool", "sp")
QUEUE_ENG = {"sync": "sp", "gpsimd": "pool", "scalar": "act"}


class Sched:
    def __init__(self, nc):
        self.nc = nc
        self.stack = ExitStack()
        self.ops = []
        self.eng_ops = {e: [] for e in ENGS}
        self.dsems = []

    def sbuf(self, name, shape, dtype):
        return self.stack.enter_context(self.nc.sbuf_tensor("sb_" + name, list(shape), dtype))

    def psum(self, name, shape, dtype):
        return self.stack.enter_context(self.nc.psum_tensor("ps_" + name, list(shape), dtype))

    def buf(self, name):
        return Buf(name)

    def dsem(self, name):
        d = DSem(name)
        self.dsems.append(d)
        return d

    def _add(self, eng, fn, reads, writes, dsem=None):
        op = Op()
        op.eng = eng
        op.fn = fn
        op.idx = len(self.ops)
        op.pos = len(self.eng_ops[eng])
        op.mark = False
        op.count = 0
        op.waits = []
        op.dsem = dsem
        op.dval = 0
        if dsem is not None:
            dsem.count += 16
            op.dval = dsem.count
        deps = []
        for b in reads:
            if b.w is not None:
                deps.append((b.w, True))
        for b in writes:
            if b.w is not None:
                deps.append((b.w, False))
            for r in b.r:
                if r is not op:
                    deps.append((r, False))
        for b in writes:
            b.w = op
            b.r = []
        for b in reads:
            if b.w is not op:
                b.r.append(op)
        op.deps = deps
        self.ops.append(op)
        self.eng_ops[eng].append(op)
        return op

    def pe(self, fn, reads=(), writes=()):
        return self._add("pe", fn, reads, writes)

    def act(self, fn, reads=(), writes=()):
        return self._add("act", fn, reads, writes)

    def dve(self, fn, reads=(), writes=()):
        return self._add("dve", fn, reads, writes)

    def pool(self, fn, reads=(), writes=()):
        return self._add("pool", fn, reads, writes)

    def dma(self, queue, fn, dsem, reads=(), writes=()):
        return self._add(QUEUE_ENG[queue], fn, reads, writes, dsem=dsem)

    def finish(self):
        nc = self.nc
        seen_pos = {e: {f: -1 for f in ENGS} for e in ENGS}
        seen_ds = {e: {} for e in ENGS}
        for op in self.ops:
            e = op.eng
            best = {}
            bestd = {}
            for (y, raw) in op.deps:
                if y.dsem is not None:
                    d = y.dsem
                    if seen_ds[e].get(d, 0) >= y.dval:
                        continue
                    if bestd.get(d, 0) < y.dval:
                        bestd[d] = y.dval
                    continue
                f = y.eng
                if f == e and op.dsem is None:
                    if e == "pe" or not raw:
                        continue
                if y.pos <= seen_pos[e][f]:
                    continue
                if f not in best or best[f].pos < y.pos:
                    best[f] = y
            for f, y in best.items():
                y.mark = True
                seen_pos[e][f] = y.pos
                op.waits.append(("c", y))
            for d, v in bestd.items():
                seen_ds[e][d] = v
                op.waits.append(("d", d, v))
        sems = {}
        for e in ENGS:
            sems[e] = self.stack.enter_context(nc.semaphore("sem_" + e))
            c = 0
            for op in self.eng_ops[e]:
                if op.mark:
                    c += 1
                    op.count = c
        for d in self.dsems:
            d.sem = self.stack.enter_context(nc.semaphore("ds_" + d.name))
        self.n_marks = {e: sum(1 for o in self.eng_ops[e] if o.mark) for e in ENGS}
        block = self.stack.enter_context(nc.Block())

        def emit(engname, eh):
            for op in self.eng_ops[engname]:
                for w in op.waits:
                    if w[0] == "c":
                        y = w[1]
                        eh.wait_ge(sems[y.eng], y.count)
                    else:
                        eh.wait_ge(w[1].sem, w[2])
                ins = op.fn(eh)
                if op.dsem is not None:
                    ins.then_inc(op.dsem.sem, 16)
                elif op.mark:
                    ins.then_inc(sems[engname], 1)
            if engname == "sp":
                for d in self.dsems:
                    if d.count > 0:
                        eh.wait_ge(d.sem, d.count)

        @block.tensor
        def _(eh):
            emit("pe", eh)

        @block.scalar
        def _(eh):
            emit("act", eh)

        @block.vector
        def _(eh):
            emit("dve", eh)

        @block.gpsimd
        def _(eh):
            emit("pool", eh)

        @block.sync
        def _(eh):
            emit("sp", eh)

        self.stack.close()


D = 1024
SEQ = 2048
NMETA = 16
NH = 16
DH = 64
DFF = 2816
NFF = DFF // 128
C = 512
NCH = SEQ // C
NSEQ = 4
EPS = 1e-6
NEG = -30000.0
PW = 4096
NPAN = 41
NW = 4
SLOT_MM = 3
P_G, P_CONV, P_WCO, P_Q, P_K, P_V, P_WSB, P_WO, P_FIN, P_FOUT = 0, 4, 12, 14, 16, 18, 20, 22, 24, 35
V_G1, V_G2, V_BG, V_CW, V_GQ, V_GK, NVEC = 0, 8, 16, 32, 56, 57, 58
C_ID, C_TRI, C_ONE, C_BLK, C_MSK, NCONST = 0, 128, 256, 384, 512, 512 + 4 * 512


def _host_consts():
    c = np.zeros((128, NCONST), np.float32)
    j = np.arange(128)[:, None]
    s = np.arange(128)[None, :]
    c[:, C_ID:C_ID + 128] = (j == s)
    c[:, C_TRI:C_TRI + 128] = -(j >= s).astype(np.float32)
    c[:, C_ONE:C_ONE + 128] = -1.0
    c[:, C_BLK:C_BLK + 128] = ((j // 64) == (s // 64))
    ql = np.arange(512)[None, :]
    for r in range(4):
        c[:, C_MSK + r * 512:C_MSK + (r + 1) * 512] = np.where(128 * r + j < ql, 0.0, NEG)
    return c


def _host_panels(w_in, w_conv_out, w_sb_out, w_o, w_ffn_in, w_ffn_out):
    pans = np.zeros((NPAN, 128, PW), np.float32)

    def colpanel(W, cols):
        sub = W[:, cols]
        n = sub.shape[1]
        return sub.reshape(8, 128, n).transpose(1, 0, 2).reshape(128, 8 * n)

    def put(i, a):
        pans[i, :, :a.shape[1]] = a

    ar = np.arange
    for pi in range(4):
        put(P_G + pi, colpanel(w_in, 6144 + pi * 512 + ar(512)))
    for j in range(8):
        cols = np.concatenate([j * 128 + ar(128), 1024 + j * 128 + ar(128), 2048 + j * 128 + ar(128)])
        put(P_CONV + j, colpanel(w_in, cols))
    for pi in range(2):
        put(P_WCO + pi, colpanel(w_conv_out, pi * 512 + ar(512)))
        put(P_Q + pi, colpanel(w_in, 3072 + pi * 512 + ar(512)))
        put(P_K + pi, colpanel(w_in, 4096 + pi * 512 + ar(512)))
        put(P_V + pi, colpanel(w_in, 5120 + pi * 512 + ar(512)))
        put(P_WSB + pi, colpanel(w_sb_out, pi * 512 + ar(512)))
        put(P_WO + pi, colpanel(w_o, pi * 512 + ar(512)))
    for i in range(11):
        cols = np.concatenate([(2 * i) * 128 + ar(128), DFF + (2 * i) * 128 + ar(128),
                               (2 * i + 1) * 128 + ar(128), DFF + (2 * i + 1) * 128 + ar(128)])
        put(P_FIN + i, colpanel(w_ffn_in, cols))
    for pi in range(6):
        nk = min(4, NFF - 4 * pi)
        blk = w_ffn_out[4 * pi * 128:(4 * pi + nk) * 128, :]
        a = blk.reshape(nk, 128, 2, 512).transpose(1, 0, 2, 3).reshape(128, nk * 2 * 512)
        put(P_FOUT + pi, a)
    return pans


def _host_vecs(norm1_g, norm2_g, b_gate, conv_w, q_norm_g, k_norm_g):
    v = np.zeros((128, NVEC), np.float32)
    v[:, V_G1:V_G1 + 8] = norm1_g.reshape(8, 128).T
    v[:, V_G2:V_G2 + 8] = norm2_g.reshape(8, 128).T
    v[:, V_BG:V_BG + 16] = b_gate.reshape(16, 128).T
    v[:, V_CW:V_CW + 24] = conv_w.reshape(3, 8, 128).transpose(2, 1, 0).reshape(128, 24)
    v[:, V_GQ] = np.tile(q_norm_g.reshape(64), 2)
    v[:, V_GK] = np.tile(k_norm_g.reshape(64), 2)
    return v


def build_program(nseq=NSEQ, nch=NCH, debug=False):
    nc = bass.Bass("TRN2", target_bir_lowering=False)
    x_d = nc.dram_tensor("x", [nseq, SEQ, D], F32, kind="ExternalInput").ap()
    meta_d = nc.dram_tensor("meta", [NMETA, D], F32, kind="ExternalInput").ap()
    vecs_d = nc.dram_tensor("vecs", [128, NVEC], F32, kind="ExternalInput").ap()
    consts_d = nc.dram_tensor("consts", [128, NCONST], F32, kind="ExternalInput").ap()
    wpan_d = nc.dram_tensor("wpan", [NPAN, 128, PW], F32, kind="ExternalInput").ap()
    out_d = nc.dram_tensor("out", [nseq, SEQ, D], F32, kind="ExternalOutput").ap()
    wscr_d = nc.dram_tensor("wscr", [NPAN, 128, PW], BF16, kind="Internal").ap()
    dbg_d = {}
    if debug:
        for nm, shp in (("d_nT", [128, 8, C]), ("d_qT", [128, 8, C]), ("d_KT", [128, 8, NMETA + SEQ]),
                        ("d_V", [128, 16, D]), ("d_uT", [128, 8, C]), ("d_sg", [128, 16, C]),
                        ("d_oT", [128, 8, C]), ("d_Vm", [NMETA, D])):
            dbg_d[nm] = nc.dram_tensor(nm, shp, BF16, kind="ExternalOutput").ap()

    S = Sched(nc)
    cst = S.sbuf("cst", [128, NCONST], BF16)
    vecs = S.sbuf("vecs", [128, NVEC], F32)
    Xn = [S.sbuf(f"Xn{i}", [128, D], F32) for i in range(2)]
    H = S.sbuf("H", [128, 4, D], F32)
    XS = [S.sbuf(f"XS{i}", [128, D], BF16) for i in range(2)]
    bufA = S.sbuf("bufA", [128, 8, C], BF16)
    mT = S.sbuf("mT", [128, 8, C], BF16)
    actS = [S.sbuf(f"actS{i}", [128, 4, C], BF16) for i in range(2)]
    bufB = S.sbuf("bufB", [128, 8, C], BF16)
    sg = S.sbuf("sg", [128, 16, C], BF16)
    qT = S.sbuf("qT", [128, 8, C], BF16)
    KT = S.sbuf("KT", [128, 8, NMETA + SEQ], BF16)
    Vc = S.sbuf("Vc", [128, 16, D], BF16)
    Vm = S.sbuf("Vm", [NMETA, D], BF16)
    U = S.sbuf("U", [128, 10240], BF16)
    W = [S.sbuf(f"W{i}", [128, PW], BF16) for i in range(NW)]
    stat = S.sbuf("stat", [128, 4, 4], F32)
    hs = S.sbuf("hs", [128, 8, 2], F32)
    hs0 = S.sbuf("hs0", [128, 8, 2], F32)
    PS = S.psum("PS", [128, 8 * 512], F32)

    ident = cst[:, C_ID:C_ID + 128]
    triN = cst[:, C_TRI:C_TRI + 128]
    onesN = cst[:, C_ONE:C_ONE + 128]
    blk = cst[:, C_BLK:C_BLK + 128]
    msk = cst[:, C_MSK:C_MSK + 2048].rearrange("p (r q) -> p r q", r=4)

    def bank(b, n=1):
        return PS[:, b * 512:(b + n) * 512]

    def bank_bf(b):
        return PS[:, b * 512:(b + 1) * 512].bitcast(BF16)

    def u_e(u):
        return U[:, u * 2048:(u + 1) * 2048].bitcast(F32)
    def u_sp(u):
        return U[:, 4096 + u * 1024:4096 + (u + 1) * 1024]
    def u_a(u):
        return U[:, 6144 + u * 1024:6144 + (u + 1) * 1024]
    def u_S(i):
        return U[:, 8192 + i * 1024:8192 + (i + 1) * 1024]
    def u_f32(slot):
        return U[:, slot * 1040:(slot + 1) * 1040].bitcast(F32)
    xc_s = [u_f32(0), u_f32(1)]
    uu_s = [u_f32(2), u_f32(3)]
    y_s = [u_f32(4), u_f32(5)]
    rs_s = [u_f32(6), u_f32(7)]
    sq_s = [U[:, 8 * 1040 + i * 512:8 * 1040 + (i + 1) * 512] for i in range(2)]
    tmp_s = [u_f32(2), u_f32(3)]
    sl_s = [u_f32(0), u_f32(1)]

    Bcst, Bvec = S.buf("cst"), S.buf("vecs")
    BXn = [S.buf("Xn0"), S.buf("Xn1")]
    BH = [S.buf(f"H{t}") for t in range(4)]
    BXS = [S.buf("XS0"), S.buf("XS1")]
    BA = [S.buf(f"A{k}") for k in range(8)]
    BB = [S.buf(f"B{k}") for k in range(8)]
    Bsg = [S.buf(f"sg{k}") for k in range(16)]
    BqT = [S.buf(f"qT{k}") for k in range(8)]
    BKm = [S.buf(f"KTm{k}") for k in range(8)]
    BKT = [[S.buf(f"KT{k}_{c}") for c in range(NCH)] for k in range(8)]
    BV = [S.buf(f"V{t}") for t in range(16)]
    BVm = S.buf("Vm")
    BW = [S.buf(f"W{i}") for i in range(NW)]
    Bm = [S.buf(f"m{k}") for k in range(8)]
    BactS = [S.buf("actS0"), S.buf("actS1")]
    Bstat = [S.buf(f"stat{i}") for i in range(4)]
    Bhs = [S.buf(f"hs{k}") for k in range(8)]
    Bhs0 = S.buf("hs0")
    PB = [S.buf(f"pb{i}") for i in range(8)]
    BU = S.buf("U")
    Be = [S.buf("e0"), S.buf("e1")]
    Bsp = [S.buf("sp0"), S.buf("sp1")]
    Ba = [S.buf("a0"), S.buf("a1")]
    BS = [S.buf("S0"), S.buf("S1")]
    Bxc = [S.buf("xc0"), S.buf("xc1")]
    Buu = [S.buf("uu0"), S.buf("uu1")]
    By = [S.buf("y0"), S.buf("y1")]
    Brs = [S.buf("rs0"), S.buf("rs1")]
    Bsq = [S.buf("sq0"), S.buf("sq1")]
    Bwscr = [S.buf(f"wscr{i}") for i in range(NPAN)]
    P1_BUFS = Bxc + Buu + By + Brs + Bsq
    P2_BUFS = Be + Bsp + Ba + BS

    def fence(olds, news):
        acc = []
        for o in olds:
            if o.w is not None:
                acc.append(o.w)
            acc.extend(o.r)
        for n in news:
            n.r = n.r + acc

    ds_c = S.dsem("c")
    ds_c2 = S.dsem("c2")
    ds_xn = [S.dsem("xn0"), S.dsem("xn1")]
    ds_h = S.dsem("h")
    ds_w = [S.dsem(f"w{i}") for i in range(NW)]
    ds_cv = [S.dsem("cv0"), S.dsem("cv1")]
    ds_scr = [S.dsem(f"scr{i}") for i in range(NW)]
    ds_out = [S.dsem(f"out{i}") for i in range(4)]
    ds_dbg = S.dsem("dbg")

    S.dma("sync", lambda e: e.dma_start(out=vecs[:], in_=vecs_d), ds_c, writes=[Bvec])
    cstage = H.rearrange("p a b -> p (a b)")[:, 0:NCONST]
    S.dma("sync", lambda e: e.dma_start(out=cstage, in_=consts_d), ds_c2, writes=BH)
    S.dve(lambda e: e.tensor_copy(out=cst[:], in_=cstage), reads=BH, writes=[Bcst])
    S.pool(lambda e: e.memset(hs0[:], 0.0), writes=[Bhs0])

    stg = [H.rearrange("p a b -> p (a b)"), sg.rearrange("p a b -> p (a b)").bitcast(F32)]
    Bstg = [BH, Bsg]
    def cv_load(i):
        s2 = i % 2
        S.dma("sync", lambda e: e.dma_start(out=stg[s2], in_=wpan_d[i]), ds_cv[s2], writes=Bstg[s2])

    def cv_cast_store(i):
        s2 = i % 2
        s4 = i % NW
        if i % 3 == 0:
            S.dve(lambda e: e.tensor_copy(out=W[s4][:], in_=stg[s2]), reads=Bstg[s2], writes=[BW[s4]])
        else:
            S.act(lambda e: e.activation(out=W[s4][:], in_=stg[s2], func=AF.Copy), reads=Bstg[s2], writes=[BW[s4]])
        S.dma("sync", lambda e: e.dma_start(out=wscr_d[i], in_=W[s4][:]), ds_scr[s4], reads=[BW[s4]], writes=[Bwscr[i]])

    cv_load(0)
    for i in range(NPAN):
        if i + 1 < NPAN:
            cv_load(i + 1)
        cv_cast_store(i)

    serial_stream = (list(range(P_G, P_G + 4)) + list(range(P_CONV, P_CONV + 8)) + [P_WCO, P_WCO + 1]
                     + [P_Q, P_Q + 1, P_K, P_K + 1, P_V, P_V + 1])
    ffn_stream = []
    for gi in range(6):
        ffn_stream += [P_FIN + 2 * gi] + ([P_FIN + 2 * gi + 1] if 2 * gi + 1 < 11 else [])
        if gi >= 1:
            ffn_stream += [P_FOUT + gi - 1]
    ffn_stream += [P_FOUT + 5]
    dense_stream = [P_WO, P_WO + 1] + ffn_stream
    meta_stream = list(range(P_CONV, P_CONV + 8)) + [P_K, P_K + 1, P_V, P_V + 1]
    stream = list(meta_stream)
    for ci in range(nseq * nch):
        stream += serial_stream + (dense_stream if ci > 0 else []) + [P_WSB, P_WSB + 1]
    stream += dense_stream
    st = {"issued": 0, "next": 0}

    def issue_to(n):
        while st["issued"] < min(n, len(stream)):
            i = st["issued"]
            pid = stream[i]
            s4 = i % NW
            S.dma("sync", lambda e, pid=pid, s4=s4: e.dma_start(out=W[s4][:], in_=wscr_d[pid]), ds_w[s4],
                  reads=[Bwscr[pid]], writes=[BW[s4]])
            st["issued"] += 1

    def next_panel(expect):
        i = st["next"]
        assert stream[i] == expect, (i, stream[i], expect)
        issue_to(i + NW)
        st["next"] += 1
        s4 = i % NW
        return W[s4], BW[s4]

    rot = {"bank": 0, "xn": 0, "xs": 0, "stat": 0, "p1": 0}

    def alloc_banks(n):
        if rot["bank"] + n > 8:
            rot["bank"] = 0
        b = rot["bank"]
        rot["bank"] = (b + n) % 8
        return b

    def nxt(key, m=2):
        v = rot[key]
        rot[key] = (v + 1) % m
        return v

    def norm_transpose(src, src_bufs, gcol, dst, dst_bufs, off, rows):
        si = nxt("stat", 4)
        xi = nxt("xs")
        stt = stat[:, si, :]
        S.pool(lambda e: e.memset(stt[:rows, 0:1], 0.0), writes=[Bstat[si]])
        S.act(lambda e: e.activation(out=XS[xi][:rows, :], in_=src, func=AF.Square, accum_out=stt[:rows, 0:1]),
              reads=src_bufs + [Bstat[si]], writes=[BXS[xi], Bstat[si]])
        S.act(lambda e: e.activation(out=stt[:rows, 1:2], in_=stt[:rows, 0:1], func=AF.Ln, scale=1.0 / D, bias=EPS),
              reads=[Bstat[si]], writes=[Bstat[si]])
        S.act(lambda e: e.activation(out=stt[:rows, 2:3], in_=stt[:rows, 1:2], func=AF.Exp, scale=-0.5),
              reads=[Bstat[si]], writes=[Bstat[si]])
        S.dve(lambda e: e.tensor_scalar(out=XS[xi][:rows, :], in0=src, scalar1=stt[:rows, 2:3], scalar2=None, op0=ALU.mult),
              reads=src_bufs + [Bstat[si]], writes=[BXS[xi]])
        b = alloc_banks(1)
        pt = bank_bf(b)
        for k in range(8):
            S.pe(lambda e, k=k: e.transpose(out=pt[:, k * 128:k * 128 + rows], in_=XS[xi][:rows, k * 128:(k + 1) * 128],
                                            identity=ident[:rows, :rows]),
                 reads=[BXS[xi], Bcst], writes=[PB[b]])
        ptv = pt.rearrange("p (k t) -> p k t", k=8)[:, :, 0:rows]
        S.dve(lambda e: e.tensor_tensor(out=dst[:, :, off:off + rows], in0=ptv,
                                        in1=gcol.unsqueeze(2).to_broadcast([128, 8, rows]), op=ALU.mult),
              reads=[PB[b], Bvec], writes=dst_bufs)

    def fm_matmul(b, lhs_of_k, rhs_of_k, nk, ntok, reads):
        for k in range(nk):
            S.pe(lambda e, k=k: e.matmul(bank(b)[:, 0:ntok], lhsT=lhs_of_k(k), rhs=rhs_of_k(k), start=(k == 0), stop=(k == nk - 1)),
                 reads=reads, writes=[PB[b]])

    hn_pending = []

    def head_norm(b, gcolumn, lnbias, dst, dst_bufs, ntok):
        i = nxt("p1")
        sq, rs = sq_s[i][:, 0:ntok], rs_s[i][:, 0:ntok]
        S.act(lambda e: e.activation(out=sq, in_=bank(b)[:, 0:ntok], func=AF.Square), reads=[PB[b]], writes=[Bsq[i]])
        b2 = alloc_banks(1)
        S.pe(lambda e: e.matmul(bank(b2)[:, 0:ntok], lhsT=blk, rhs=sq, start=True, stop=True), reads=[Bsq[i], Bcst], writes=[PB[b2]])

        def stage_b():
            S.act(lambda e: e.activation(out=rs, in_=bank(b2)[:, 0:ntok], func=AF.Ln, scale=1.0 / DH, bias=EPS), reads=[PB[b2]], writes=[Brs[i]])
            S.act(lambda e: e.activation(out=rs, in_=rs, func=AF.Exp, scale=-0.5, bias=lnbias), reads=[Brs[i]], writes=[Brs[i]])
            S.dve(lambda e: e.scalar_tensor_tensor(out=dst, in0=bank(b)[:, 0:ntok], scalar=gcolumn, in1=rs, op0=ALU.mult, op1=ALU.mult),
                  reads=[PB[b], Brs[i], Bvec], writes=dst_bufs)
        hn_flush()
        hn_pending.append(stage_b)

    def hn_flush():
        while hn_pending:
            hn_pending.pop(0)()

    def phase1(ntok, meta, kt_off, kt_bufs, vtile0, seq_first):
        nT = bufA
        rhsA = lambda k: nT[:, k, 0:ntok]
        if not meta:
            for pi in range(4):
                Wp, BWp = next_panel(P_G + pi)
                wv = Wp.rearrange("p (k c) -> p k c", k=8)
                for jj in range(4):
                    j = pi * 4 + jj
                    b = alloc_banks(1)
                    fm_matmul(b, lambda k, jj=jj, wv=wv: wv[:, k, jj * 128:(jj + 1) * 128], rhsA, 8, ntok, BA + [BWp])
                    S.act(lambda e, b=b, j=j: e.activation(out=sg[:, j, :], in_=bank(b), func=AF.Sigmoid, bias=vecs[:, V_BG + j:V_BG + j + 1]),
                          reads=[PB[b], Bvec], writes=[Bsg[j]])
        for j in range(8):
            Wp, BWp = next_panel(P_CONV + j)
            wv = Wp[:, 0:8 * 384].rearrange("p (k c) -> p k c", k=8)
            b0 = alloc_banks(3)
            tiles = (1, 2) if meta else (0, 1, 2)
            for t in tiles:
                fm_matmul(b0 + t, lambda k, t=t, wv=wv: wv[:, k, t * 128:(t + 1) * 128], rhsA, 8, ntok, BA + [BWp])
            i = nxt("p1")
            xc, uu, y = xc_s[i], uu_s[i], y_s[i]
            S.act(lambda e, b0=b0, xc=xc: e.activation(out=xc[:, 0:ntok], in_=bank(b0 + 1)[:, 0:ntok], func=AF.Copy), reads=[PB[b0 + 1]], writes=[Bxc[i]])
            if not meta:
                src_h, src_b = (hs0, Bhs0) if seq_first else (hs, Bhs[j])
                S.pool(lambda e, uu=uu, src_h=src_h, j=j: e.tensor_copy(out=uu[:, 0:2], in_=src_h[:, j, :]), reads=[src_b], writes=[Buu[i]])
            S.dve(lambda e, b0=b0, xc=xc, uu=uu: e.tensor_tensor(out=uu[:, 2:2 + ntok], in0=bank(b0 + 2)[:, 0:ntok], in1=xc[:, 0:ntok], op=ALU.mult),
                  reads=[PB[b0 + 2], Bxc[i]], writes=[Buu[i]])
            dst_h, dst_b = (hs0, Bhs0) if meta else (hs, Bhs[j])
            S.pool(lambda e, uu=uu, dst_h=dst_h, j=j: e.tensor_copy(out=dst_h[:, j, :], in_=uu[:, ntok:ntok + 2]), reads=[Buu[i]], writes=[dst_b])
            if meta:
                continue
            cw = lambda tap, j=j: vecs[:, V_CW + j * 3 + tap:V_CW + j * 3 + tap + 1]
            S.dve(lambda e, uu=uu, y=y, cw=cw: e.tensor_scalar(out=y[:, 0:ntok], in0=uu[:, 2:2 + ntok], scalar1=cw(2), scalar2=None, op0=ALU.mult),
                   reads=[Buu[i], Bvec], writes=[By[i]])
            S.dve(lambda e, uu=uu, y=y, cw=cw: e.scalar_tensor_tensor(out=y[:, 0:ntok], in0=uu[:, 1:1 + ntok], scalar=cw(1), in1=y[:, 0:ntok], op0=ALU.mult, op1=ALU.add),
                   reads=[Buu[i], By[i], Bvec], writes=[By[i]])
            S.dve(lambda e, uu=uu, y=y, cw=cw: e.scalar_tensor_tensor(out=y[:, 0:ntok], in0=uu[:, 0:ntok], scalar=cw(0), in1=y[:, 0:ntok], op0=ALU.mult, op1=ALU.add),
                   reads=[Buu[i], By[i], Bvec], writes=[By[i]])
            S.dve(lambda e, b0=b0, y=y, j=j: e.tensor_tensor(out=bufB[:, j, 0:ntok], in0=bank(b0)[:, 0:ntok], in1=y[:, 0:ntok], op=ALU.mult),
                  reads=[PB[b0], By[i]], writes=[BB[j]])
        if not meta:
            for pi in range(2):
                Wp, BWp = next_panel(P_WCO + pi)
                wv = Wp.rearrange("p (k c) -> p k c", k=8)
                for jj in range(4):
                    j = pi * 4 + jj
                    b = alloc_banks(1)
                    fm_matmul(b, lambda k, jj=jj, wv=wv: wv[:, k, jj * 128:(jj + 1) * 128], lambda k: bufB[:, k, 0:ntok], 8, ntok, BB + [BWp])
                    S.dve(lambda e, b=b, j=j: e.tensor_tensor(out=sg[:, j, :], in0=bank(b), in1=sg[:, j, :], op=ALU.mult),
                          reads=[PB[b], Bsg[j]], writes=[Bsg[j]])
            for pi in range(2):
                Wp, BWp = next_panel(P_Q + pi)
                wv = Wp.rearrange("p (k c) -> p k c", k=8)
                for jj in range(4):
                    j = pi * 4 + jj
                    b = alloc_banks(1)
                    fm_matmul(b, lambda k, jj=jj, wv=wv: wv[:, k, jj * 128:(jj + 1) * 128], rhsA, 8, ntok, BA + [BWp])
                    head_norm(b, vecs[:, V_GQ:V_GQ + 1], float(np.log(DH ** -0.5)), qT[:, j, 0:ntok], [BqT[j]], ntok)
        for pi in range(2):
            Wp, BWp = next_panel(P_K + pi)
            wv = Wp.rearrange("p (k c) -> p k c", k=8)
            for jj in range(4):
                j = pi * 4 + jj
                b = alloc_banks(1)
                fm_matmul(b, lambda k, jj=jj, wv=wv: wv[:, k, jj * 128:(jj + 1) * 128], rhsA, 8, ntok, BA + [BWp])
                head_norm(b, vecs[:, V_GK:V_GK + 1], 0.0, KT[:, j, kt_off:kt_off + ntok], [kt_bufs[j]], ntok)
        hn_flush()
        ntt = 1 if meta else ntok // 128
        rows = ntok if meta else 128
        vb = alloc_banks(8) if not meta else alloc_banks(2)
        for pi in range(2):
            Wp, BWp = next_panel(P_V + pi)
            wv = Wp.rearrange("p (k c) -> p k c", k=8)
            for tt in range(ntt):
                b = vb + tt * 2 + pi
                for k in range(8):
                    S.pe(lambda e, b=b, k=k, tt=tt, wv=wv: e.matmul(bank(b)[0:rows, :], lhsT=nT[:, k, tt * 128:tt * 128 + rows], rhs=wv[:, k, :],
                                                                  start=(k == 0), stop=(k == 7)),
                         reads=BA + [BWp], writes=[PB[b]])
                if meta:
                    S.dve(lambda e, b=b, pi=pi: e.tensor_copy(out=Vm[:, pi * 512:(pi + 1) * 512], in_=bank(b)[0:rows, :]), reads=[PB[b]], writes=[BVm])
                elif (tt + pi) % 2 == 0:
                    S.dve(lambda e, b=b, pi=pi, tt=tt: e.tensor_copy(out=Vc[:, vtile0 + tt, pi * 512:(pi + 1) * 512], in_=bank(b)), reads=[PB[b]], writes=[BV[vtile0 + tt]])
                else:
                    S.act(lambda e, b=b, pi=pi, tt=tt: e.activation(out=Vc[:, vtile0 + tt, pi * 512:(pi + 1) * 512], in_=bank(b), func=AF.Copy), reads=[PB[b]], writes=[BV[vtile0 + tt]])
    def dense_items(bseq, c, dbanks, nxt_chunk=None):
        rr = {"i": 0}

        def nb():
            b = dbanks[rr["i"] % len(dbanks)]
            rr["i"] += 1
            return b

        for pi in range(2):
            hold = {}
            for tt in range(4):
                b = nb()

                def mm(k, b=b, pi=pi, tt=tt, hold=hold):
                    if "w" not in hold:
                        hold["w"] = next_panel(P_WO + pi)
                    Wp, BWp = hold["w"]
                    wv = Wp.rearrange("p (k c) -> p k c", k=8)
                    S.pe(lambda e: e.matmul(bank(b), lhsT=mT[:, k, tt * 128:(tt + 1) * 128], rhs=wv[:, k, :], start=(k == 0), stop=(k == 7)),
                         reads=Bm + [BWp], writes=[PB[b]])
                pe = [(lambda k=k, mm=mm: mm(k)) for k in range(8)]

                def post(b=b, pi=pi, tt=tt):
                    S.dve(lambda e: e.tensor_tensor(out=H[:, tt, pi * 512:(pi + 1) * 512], in0=bank(b), in1=H[:, tt, pi * 512:(pi + 1) * 512], op=ALU.add),
                          reads=[PB[b], BH[tt]], writes=[BH[tt]])
                yield dict(mms=pe, post=post, bank=b, flush=False, pre=None)
        def norm_items(src_of, src_bufs_of, gcol, dst, dst_bufs, loader=None):
            holds = [dict() for _ in range(4)]

            def make_a(tt):
                def pre():
                    if loader is not None:
                        loader(tt)
                    si = nxt("stat", 4)
                    xs = nxt("xs")
                    holds[tt]["xs"] = xs
                    stt = stat[:, si, :]
                    src = src_of(tt)
                    sb = src_bufs_of(tt)
                    S.dve(lambda e: e.memset(stt[:, 0:1], 0.0), writes=[Bstat[si]])
                    S.act(lambda e: e.activation(out=XS[xs][:], in_=src, func=AF.Square, accum_out=stt[:, 0:1]),
                          reads=sb + [Bstat[si]], writes=[BXS[xs], Bstat[si]])
                    S.act(lambda e: e.activation(out=stt[:, 1:2], in_=stt[:, 0:1], func=AF.Ln, scale=1.0 / D, bias=EPS), reads=[Bstat[si]], writes=[Bstat[si]])
                    S.act(lambda e: e.activation(out=stt[:, 2:3], in_=stt[:, 1:2], func=AF.Exp, scale=-0.5), reads=[Bstat[si]], writes=[Bstat[si]])
                    S.dve(lambda e: e.tensor_scalar(out=XS[xs][:], in0=src, scalar1=stt[:, 2:3], scalar2=None, op0=ALU.mult),
                          reads=sb + [Bstat[si]], writes=[BXS[xs]])
                return dict(mms=[], post=None, bank=None, flush=(tt == 0), pre=pre)

            def make_b(tt):
                b = nb()

                def mm(k):
                    xs = holds[tt]["xs"]
                    pt = bank_bf(b)
                    S.pe(lambda e: e.transpose(out=pt[:, k * 128:(k + 1) * 128], in_=XS[xs][:, k * 128:(k + 1) * 128], identity=ident),
                         reads=[BXS[xs], Bcst], writes=[PB[b]])

                def post():
                    ptv = bank_bf(b).rearrange("p (k t) -> p k t", k=8)
                    S.dve(lambda e: e.tensor_tensor(out=dst[:, :, tt * 128:(tt + 1) * 128], in0=ptv,
                                                    in1=gcol.unsqueeze(2).to_broadcast([128, 8, 128]), op=ALU.mult),
                          reads=[PB[b], Bvec], writes=dst_bufs)
                return dict(mms=[(lambda k=k: mm(k)) for k in range(8)], post=post, bank=b, flush=False, pre=None)

            for kind, tt in (("A", 0), ("A", 1), ("B", 0), ("A", 2), ("B", 1), ("A", 3), ("B", 2), ("B", 3)):
                yield make_a(tt) if kind == "A" else make_b(tt)

        yield from norm_items(lambda tt: H[:, tt, :], lambda tt: [BH[tt]], vecs[:, V_G2:V_G2 + 8], mT, Bm)
        def ffn_in_items(gi):
            nk = min(4, NFF - 4 * gi)
            sl = gi % 2
            hold = {}
            for jj in range(nk):
                for t in range(2):
                    b = nb()

                    def mm(k, b=b, jj=jj, t=t, gi=gi, hold=hold):
                        if jj % 2 == 0 and t == 0 and k == 0:
                            hold["w"] = next_panel(P_FIN + 2 * gi + jj // 2)
                        Wp, BWp = hold["w"]
                        wv = Wp.rearrange("p (k c) -> p k c", k=8)
                        col = ((jj % 2) * 2 + t) * 128
                        S.pe(lambda e: e.matmul(bank(b), lhsT=wv[:, k, col:col + 128], rhs=mT[:, k, :], start=(k == 0), stop=(k == 7)),
                             reads=Bm + [BWp], writes=[PB[b]])
                    pe = [(lambda k=k, mm=mm: mm(k)) for k in range(8)]

                    def post(b=b, jj=jj, t=t, sl=sl, hold=hold):
                        if t == 0:
                            xi = nxt("xn")
                            hold["xi"] = xi
                            tmp = Xn[xi][:, 0:C]
                            S.act(lambda e: e.activation(out=tmp, in_=bank(b), func=AF.Exp, scale=-1.0), reads=[PB[b]], writes=[BXn[xi]])
                            S.act(lambda e: e.activation(out=tmp, in_=tmp, func=AF.Ln, bias=1.0), reads=[BXn[xi]], writes=[BXn[xi]])
                            S.act(lambda e: e.activation(out=tmp, in_=tmp, func=AF.Exp, scale=-1.0), reads=[BXn[xi]], writes=[BXn[xi]])
                            S.dve(lambda e: e.tensor_tensor(out=tmp, in0=bank(b), in1=tmp, op=ALU.mult), reads=[PB[b], BXn[xi]], writes=[BXn[xi]])
                        else:
                            xi = hold["xi"]
                            tmp = Xn[xi][:, 0:C]
                            S.dve(lambda e: e.tensor_tensor(out=actS[sl][:, jj, :], in0=bank(b), in1=tmp, op=ALU.mult),
                                  reads=[PB[b], BXn[xi]], writes=[BactS[sl]])
                    yield dict(mms=pe, post=post, bank=b, flush=(gi == 0 and jj == 0 and t == 0), pre=None)
        def fout_items(gi):
            nk = min(4, NFF - 4 * gi)
            sl = gi % 2
            hold2 = {}
            for tt in range(4):
                for half in range(2):
                    b = nb()

                    def mm(kk, b=b, tt=tt, half=half, gi=gi, nk=nk, sl=sl, hold2=hold2):
                        if "w" not in hold2:
                            hold2["w"] = next_panel(P_FOUT + gi)
                        Wp, BWp = hold2["w"]
                        wv = Wp[:, 0:nk * 1024].rearrange("p (s c) -> p s c", c=512)
                        S.pe(lambda e: e.matmul(bank(b), lhsT=actS[sl][:, kk, tt * 128:(tt + 1) * 128], rhs=wv[:, kk * 2 + half, :],
                                                start=(kk == 0), stop=(kk == nk - 1)),
                             reads=[BactS[sl], BWp], writes=[PB[b]])
                    pe = [(lambda kk=kk, mm=mm: mm(kk)) for kk in range(nk)]

                    def post(b=b, tt=tt, half=half, gi=gi):
                        S.dve(lambda e: e.tensor_tensor(out=H[:, tt, half * 512:(half + 1) * 512], in0=bank(b), in1=H[:, tt, half * 512:(half + 1) * 512], op=ALU.add),
                              reads=[PB[b], BH[tt]], writes=[BH[tt]])
                        if gi == 5 and half == 1:
                            r0 = c * C + tt * 128
                            S.dma("sync", lambda e: e.dma_start(out=out_d[bseq, r0:r0 + 128, :], in_=H[:, tt, :]), ds_out[tt], reads=[BH[tt]])
                    yield dict(mms=pe, post=post, bank=b, flush=(tt == 0 and half == 0), pre=None)

        for gi in range(6):
            yield from ffn_in_items(gi)
            if gi >= 1:
                yield from fout_items(gi - 1)
        yield from fout_items(5)

        if nxt_chunk is not None:
            nb_, nc_ = nxt_chunk
            xslot = {}

            def loader(tt):
                xi = nxt("xn")
                xslot[tt] = xi
                r0 = nc_ * C + tt * 128
                S.dma("sync", lambda e: e.dma_start(out=Xn[xi][:], in_=x_d[nb_, r0:r0 + 128, :]), ds_xn[xi], writes=[BXn[xi]])
            yield from norm_items(lambda tt: Xn[xslot[tt]][:], lambda tt: [BXn[xslot[tt]]], vecs[:, V_G1:V_G1 + 8], bufA, BA, loader)

    class DenseFeeder:
        def __init__(self, gen, n_mm):
            self.gen = gen
            self.left = n_mm
            self.pending = {}
            self.cur = None
            self.idx = 0

        def feed(self, n):
            while n > 0 and self.gen is not None:
                if self.cur is None:
                    try:
                        it = next(self.gen)
                    except StopIteration:
                        self.gen = None
                        self.flush()
                        return
                    if it["flush"]:
                        self.flush()
                    if it["bank"] is not None and it["bank"] in self.pending:
                        self.pending.pop(it["bank"])()
                    if it["pre"] is not None:
                        it["pre"]()
                    self.cur = it
                    self.idx = 0
                mms = self.cur["mms"]
                take = min(n, len(mms) - self.idx)
                for f in mms[self.idx:self.idx + take]:
                    f()
                self.idx += take
                n -= take
                self.left -= take
                if self.idx == len(mms):
                    if self.cur["post"] is not None:
                        for b in list(self.pending):
                            self.pending.pop(b)()
                        self.pending[self.cur["bank"]] = self.cur["post"]
                    self.cur = None

        def flush(self):
            for b in list(self.pending):
                self.pending.pop(b)()

        def drain(self):
            while self.gen is not None:
                self.feed(64)
            self.flush()

    N_DENSE_MM = 64 + 32 + 352 + 176

    def phase2(c, feeder):
        fence(P1_BUFS, P2_BUFS)
        blocks = [("d", 4 * c + r, r) for r in (3, 2, 1, 0)] + [("f", kt, 0) for kt in range(4 * c - 1, -1, -1)] + [("m", 0, 0)]
        nb = len(blocks)
        steps = [(j, n) for j in range(8) for n in range(nb)]
        ns = len(steps)
        ZD = [0, 2]
        OB = [4, 5]

        def geom(n):
            kind, kt, r = blocks[n]
            if kind == "m":
                return kind, kt, r, NMETA, 0, 0
            return kind, kt, r, 128, NMETA + kt * 128, (128 * r if kind == "d" else 0)

        def v3(ap2):
            return ap2.rearrange("p (h q) -> p h q", h=2)

        def QK(g):
            j, n = steps[g]
            kind, kt, r, nk, koff, q0 = geom(n)
            z = ZD[g % 2]
            kb = [BKm[j]] if kind == "m" else [BKT[j][kt // 4]]
            for half in range(2):
                lo = 64 * half
                b = z + half
                S.pe(lambda e, b=b, lo=lo: e.matmul(bank(b)[0:nk, q0:C], lhsT=KT[lo:lo + 64, j, koff:koff + nk], rhs=qT[lo:lo + 64, j, q0:C],
                                                    start=True, stop=(kind != "d")),
                     reads=kb + [BqT[j]], writes=[PB[b]])
            if kind == "d":
                for half in range(2):
                    b = z + half
                    S.pe(lambda e, b=b: e.matmul(bank(b)[:, q0:q0 + 128], lhsT=ident, rhs=msk[:, 0, 0:128], start=False, stop=True, skip_group_check=True),
                         reads=[Bcst], writes=[PB[b]])

        def E1(g):
            j, n = steps[g]
            nk, q0 = geom(n)[3], geom(n)[5]
            u = g % 2
            zd = v3(bank(ZD[u], 2))
            S.act(lambda e: e.activation(out=v3(u_e(u))[0:nk, :, q0:C], in_=zd[0:nk, :, q0:C], func=AF.Exp), reads=[PB[ZD[u]], PB[ZD[u] + 1]], writes=[Be[u]])

        def SP(g):
            j, n = steps[g]
            nk, q0 = geom(n)[3], geom(n)[5]
            u = g % 2
            S.act(lambda e: e.activation(out=v3(u_sp(u))[0:nk, :, q0:C], in_=v3(u_e(u))[0:nk, :, q0:C], func=AF.Ln, bias=1.0), reads=[Be[u]], writes=[Bsp[u]])

        def TO(g):
            j, n = steps[g]
            nk, q0 = geom(n)[3], geom(n)[5]
            u = g % 2
            si = n % 2
            if n == 0:
                for i in range(2):
                    S.dve(lambda e, i=i: e.memset(u_S(i), 0.0), writes=[BS[i]])
            for half in range(2):
                b = ZD[u] + half
                S.pe(lambda e, b=b, half=half: e.matmul(bank(b)[0:nk, q0:C], lhsT=triN[0:nk, 0:nk], rhs=v3(u_sp(u))[0:nk, half, q0:C],
                                                       start=False, stop=(n == 0), skip_group_check=True),
                     reads=[Bsp[u], Bcst], writes=[PB[b]])
                if n > 0:
                    S.pe(lambda e, b=b, half=half: e.matmul(bank(b)[0:nk, q0:C], lhsT=onesN[:, 0:nk], rhs=v3(u_S(si))[:, half, q0:C],
                                                           start=False, stop=True, skip_group_check=True),
                         reads=[BS[si], Bcst], writes=[PB[b]])
            if n + 1 < nb:
                if n == 0:
                    S.dve(lambda e: e.tensor_copy(out=v3(u_S(1))[:, :, q0:C], in_=v3(u_sp(u))[:, :, q0:C]), reads=[Bsp[u]], writes=[BS[1]])
                else:
                    S.dve(lambda e: e.tensor_tensor(out=v3(u_S(1 - si))[:, :, q0:C], in0=v3(u_S(si))[:, :, q0:C], in1=v3(u_sp(u))[:, :, q0:C], op=ALU.add),
                          reads=[BS[si], Bsp[u]], writes=[BS[1 - si]])

        def E2(g):
            j, n = steps[g]
            nk, q0 = geom(n)[3], geom(n)[5]
            u = g % 2
            zd = v3(bank(ZD[u], 2))
            S.act(lambda e: e.activation(out=v3(u_a(u))[0:nk, :, q0:C], in_=zd[0:nk, :, q0:C], func=AF.Exp), reads=[PB[ZD[u]], PB[ZD[u] + 1]], writes=[Ba[u]])

        def AV(g):
            j, n = steps[g]
            kind, kt, r, nk, koff, q0 = geom(n)
            u = g % 2
            if kind == "m":
                lhs, rb = Vm[:, j * 128:(j + 1) * 128], [BVm]
            else:
                lhs, rb = Vc[:, kt, j * 128:(j + 1) * 128], [BV[kt]]
            for half in range(2):
                ob = OB[half]
                S.pe(lambda e, half=half, ob=ob: e.matmul(bank(ob)[:, q0:C], lhsT=lhs, rhs=v3(u_a(u))[0:nk, half, q0:C],
                                                         start=(n == 0), stop=(n == nb - 1), skip_group_check=True),
                     reads=rb + [Ba[u]], writes=[PB[ob]])
            if n == nb - 1:
                for half in range(2):
                    lo = 64 * half
                    ob = OB[half]
                    S.dve(lambda e, lo=lo, ob=ob: e.tensor_copy(out=bufB[lo:lo + 64, j, :], in_=bank(ob)[lo:lo + 64, :]), reads=[PB[ob]], writes=[BB[j]])

        QK(0)
        E1(0)
        SP(0)
        if ns > 1:
            QK(1)
        for g in range(ns):
            TO(g)
            if g + 1 < ns:
                E1(g + 1)
            need = 0
            if feeder is not None and feeder.gen is not None:
                need = -(-feeder.left // (ns - g))
                feeder.feed(min(need, SLOT_MM))
            E2(g)
            if g + 1 < ns:
                SP(g + 1)
            AV(g)
            if g + 2 < ns:
                QK(g + 2)
            if need > SLOT_MM and feeder.gen is not None:
                feeder.feed(need - SLOT_MM)

    def phase3a():
        fence(P2_BUFS, P1_BUFS)
        for pi in range(2):
            Wp, BWp = next_panel(P_WSB + pi)
            wv = Wp.rearrange("p (k c) -> p k c", k=8)
            for jj in range(4):
                j = pi * 4 + jj
                b = alloc_banks(1)
                fm_matmul(b, lambda k, jj=jj, wv=wv: wv[:, k, jj * 128:(jj + 1) * 128], lambda k: bufB[:, k, :], 8, C, BB + [BWp])
                i = nxt("p1")
                tmp = tmp_s[i][:, 0:C]
                S.dve(lambda e, b=b, j=j, tmp=tmp: e.tensor_tensor(out=tmp, in0=bank(b), in1=sg[:, 8 + j, :], op=ALU.mult),
                      reads=[PB[b], Bsg[8 + j]], writes=[Buu[i]])
                S.pool(lambda e, j=j, tmp=tmp: e.tensor_tensor(out=mT[:, j, :], in0=tmp, in1=sg[:, j, :], op=ALU.add),
                       reads=[Buu[i], Bsg[j]], writes=[Bm[j]])

    def dump(name, src, reads):
        if debug:
            S.dma("sync", lambda e: e.dma_start(out=dbg_d[name], in_=src), ds_dbg, reads=reads)

    xi = nxt("xn")
    S.dma("sync", lambda e: e.dma_start(out=Xn[xi][0:NMETA, :], in_=meta_d), ds_xn[xi], writes=[BXn[xi]])
    norm_transpose(Xn[xi][0:NMETA, :], [BXn[xi]], vecs[:, V_G1:V_G1 + 8], bufA, BA, 0, NMETA)
    phase1(NMETA, True, 0, BKm, 0, False)
    rot["bank"] = 0

    DB = [6, 7]
    pending = None
    p0_done = False
    for bseq in range(nseq):
        for c in range(nch):
            t0 = c * C
            if not p0_done:
                for tt in range(4):
                    xi = nxt("xn")
                    r0 = t0 + tt * 128
                    S.dma("sync", lambda e, bseq=bseq, r0=r0, xi=xi: e.dma_start(out=Xn[xi][:], in_=x_d[bseq, r0:r0 + 128, :]), ds_xn[xi], writes=[BXn[xi]])
                    norm_transpose(Xn[xi][:], [BXn[xi]], vecs[:, V_G1:V_G1 + 8], bufA, BA, tt * 128, 128)
            p0_done = False
            first = (bseq == 0 and c == 0)
            if debug and first:
                dump("d_nT", bufA[:], BA)
            phase1(C, False, NMETA + t0, [BKT[j][c] for j in range(8)], 4 * c, c == 0)
            rot["bank"] = 0
            if debug and first:
                dump("d_qT", qT[:], BqT)
                dump("d_uT", bufB[:], BB)
                dump("d_sg", sg[:], Bsg)
                dump("d_Vm", Vm[:], [BVm])
            feeder = None
            if pending is not None:
                ci = bseq * nch + c + 1
                nxc = (ci // nch, ci % nch) if ci < nseq * nch else None
                feeder = DenseFeeder(dense_items(pending[0], pending[1], DB, nxc), N_DENSE_MM + (32 if nxc else 0))
                p0_done = nxc is not None
            phase2(c, feeder)
            if feeder is not None:
                feeder.drain()
            rot["bank"] = 0
            if debug and first:
                dump("d_oT", bufB[:], BB)
            if debug and bseq == 0 and c == nch - 1:
                dump("d_KT", KT[:], BKm + [BKT[j][cc] for j in range(8) for cc in range(nch)])
                dump("d_V", Vc[:], BV)
            phase3a()
            rot["bank"] = 0
            S.dma("sync", lambda e, bseq=bseq, t0=t0: e.dma_start(out=H[:], in_=x_d[bseq, t0:t0 + C, :].rearrange("(t p) d -> p t d", p=128)),
                  ds_h, writes=BH)
            pending = (bseq, c)
    feeder = DenseFeeder(dense_items(pending[0], pending[1], [0, 1, 2, 3, 4, 5, 6, 7]), N_DENSE_MM)
    feeder.drain()
    assert st["next"] == len(stream), (st["next"], len(stream))
    S.finish()
    return nc, S


_CACHE = {}


def kernel(x, meta_tokens, norm1_g, w_in, b_gate, conv_w, w_conv_out, q_norm_g, k_norm_g, w_sb_out, w_o,
           norm2_g, w_ffn_in, w_ffn_out):
    x = np.ascontiguousarray(np.asarray(x, dtype=np.float32))
    f = lambda a: np.asarray(a, dtype=np.float32)
    pans = _host_panels(f(w_in)[0], f(w_conv_out)[0], f(w_sb_out)[0], f(w_o)[0], f(w_ffn_in)[0], f(w_ffn_out)[0])
    vecs = _host_vecs(f(norm1_g)[0], f(norm2_g)[0], f(b_gate)[0], f(conv_w)[0], f(q_norm_g)[0], f(k_norm_g)[0])
    consts = _host_consts()
    meta = np.ascontiguousarray(f(meta_tokens))
    if "nc" not in _CACHE:
        _CACHE["nc"] = build_program()[0]
    nc = _CACHE["nc"]
    ncores = 8
    in_maps = [{"x": x[i * NSEQ:(i + 1) * NSEQ], "meta": meta, "vecs": vecs, "consts": consts, "wpan": pans} for i in range(ncores)]
    res = run_bass_kernel_spmd(nc, in_maps, core_ids=list(range(ncores)))
    return np.concatenate([r["out"] for r in res.results], axis=0)
```
